# Optimizing a Trainium2 kernel written in Bass

```python
import math
import jax, jax.numpy as jnp
from jax import lax
import numpy as np

D_MODEL = 1024
BATCH = 8
SEQ = 2048
DEPTH = 1
DEC_BATCH = 128
DEC_SEQ = 1
PAST_LEN = 16384
PAGE_SIZE = 128

MIX_WIDTH = D_MODEL
M_HEADS = 4
M_HEAD_DIM = MIX_WIDTH // 2 // M_HEADS
M_WIDTH = M_HEADS * M_HEAD_DIM
R_HEADS = 4
R_HEAD_DIM = MIX_WIDTH // 2 // R_HEADS
R_WIDTH = R_HEADS * R_HEAD_DIM
CONV_W = 4
CHUNK = 128
N_MEM = 256
X_HEADS = 4
X_HEAD_DIM = D_MODEL // X_HEADS
D_FF = -(-8 * D_MODEL // (3 * 256)) * 256
ROPE_THETA = 10000.0
EPS = 1e-6
IN_COLS = 4 * M_WIDTH + 2 * M_HEADS + 4 * R_WIDTH

kernel_name = "hymba_mlstm_retention_decoder_step"


def rmsnorm(x, g):
    xf = x.astype(jnp.float32)
    xf = xf * lax.rsqrt(jnp.mean(xf * xf, axis=-1, keepdims=True) + EPS)
    return (xf * g.astype(jnp.float32)).astype(x.dtype)


def head_rmsnorm(h, g):
    B, T, H, D = h.shape
    h = h * lax.rsqrt(jnp.mean(h * h, axis=-1, keepdims=True) + EPS)
    return h.reshape(B, T, H * D) * g.astype(jnp.float32)


def short_conv(u, buf, w, b):
    T = u.shape[1]
    full = jnp.concatenate([buf.astype(u.dtype), u], axis=1)
    out = b + sum(full[:, j:j + T] * w[j] for j in range(CONV_W))
    return jax.nn.silu(out), full[:, T:]


def rope(x, pos):
    half = x.shape[-1] // 2
    inv = ROPE_THETA ** (-jnp.arange(half, dtype=jnp.float32) / half)
    ang = pos[:, None] * inv[None, :]
    cos = jnp.cos(ang)[None, :, None, :]
    sin = jnp.sin(ang)[None, :, None, :]
    x1, x2 = x[..., :half], x[..., half:]
    return jnp.concatenate([x1 * cos - x2 * sin, x2 * cos + x1 * sin], axis=-1)


def chunk_len(T):
    return CHUNK if T % CHUNK == 0 else T


def to_chunks(a, L):
    B, T, H, D = a.shape
    return a.reshape(B, T // L, L, H, D).transpose(1, 0, 3, 2, 4)


def gate_chunks(a, L):
    B, T, H = a.shape
    return a.reshape(B, T // L, L, H).transpose(1, 0, 3, 2)


def from_chunks(o):
    NC, B, H, L, D = o.shape
    return o.transpose(1, 0, 3, 2, 4).reshape(B, NC * L, H, D)


def mlstm_scan(q, k, v, ig, lf, C0, n0, m0):
    T = q.shape[1]
    L = chunk_len(T)
    tril = jnp.tril(jnp.ones((L, L), dtype=bool))

    def step(carry, xs):
        C, n, m = carry
        qc, kc, vc, ic, fc = xs
        b = jnp.cumsum(fc, axis=-1)
        dmat = b[..., :, None] - b[..., None, :] + ic[..., None, :]
        dmat = jnp.where(tril, dmat, -jnp.inf)
        inter = b + m[..., None]
        m_t = jnp.maximum(inter, jnp.max(dmat, axis=-1))
        wts = jnp.exp(dmat - m_t[..., None]) * jnp.einsum('bhtd,bhsd->bhts', qc, kc)
        w_in = jnp.exp(inter - m_t)
        num = jnp.einsum('bhts,bhsd->bhtd', wts, vc) + w_in[..., None] * jnp.einsum('bhvk,bhtk->bhtv', C, qc)
        den = jnp.sum(wts, axis=-1) + w_in * jnp.einsum('bhk,bhtk->bht', n, qc)
        h = num / jnp.maximum(jnp.abs(den), jnp.exp(-m_t))[..., None]
        bL = b[..., -1]
        g = bL[..., None] - b + ic
        m_new = jnp.maximum(bL + m, jnp.max(g, axis=-1))
        ws = jnp.exp(g - m_new[..., None])
        carry_scale = jnp.exp(bL + m - m_new)
        C_new = carry_scale[..., None, None] * C + jnp.einsum('bhsv,bhsk->bhvk', vc * ws[..., None], kc)
        n_new = carry_scale[..., None] * n + jnp.einsum('bhs,bhsk->bhk', ws, kc)
        return (C_new, n_new, m_new), h

    (C, n, m), h = lax.scan(step, (C0, n0, m0),
                            (to_chunks(q, L), to_chunks(k, L), to_chunks(v, L),
                             gate_chunks(ig, L), gate_chunks(lf, L)))
    return from_chunks(h), C, n, m


def retention_scan(q, k, v, S0):
    T = q.shape[1]
    L = chunk_len(T)
    lg = jnp.log1p(-jnp.exp2(-5.0 - jnp.arange(R_HEADS, dtype=jnp.float32)))
    t = jnp.arange(L, dtype=jnp.float32)
    diff = t[:, None] - t[None, :]
    decay = jnp.where(diff >= 0, jnp.exp(lg[:, None, None] * jnp.maximum(diff, 0.0)), 0.0)
    q_dec = jnp.exp(lg[:, None] * (t + 1.0))
    k_dec = jnp.exp(lg[:, None] * (L - 1.0 - t))
    chunk_dec = jnp.exp(lg * L)

    def step(S, xs):
        qc, kc, vc = xs
        o = (jnp.einsum('bhts,bhsv->bhtv', jnp.einsum('bhtd,bhsd->bhts', qc, kc) * decay, vc)
             + q_dec[..., None] * jnp.einsum('bhtk,bhkv->bhtv', qc, S))
        S = chunk_dec[:, None, None] * S + jnp.einsum('bhsk,bhsv->bhkv', kc * k_dec[..., None], vc)
        return S, o

    S, o = lax.scan(step, S0, (to_chunks(q, L), to_chunks(k, L), to_chunks(v, L)))
    return from_chunks(o), S


def mem_kv(mem, g_mem, w_ck, w_cv):
    B = mem.shape[0]
    mn = rmsnorm(mem, g_mem)
    k = (mn @ w_ck).reshape(B, N_MEM, X_HEADS, X_HEAD_DIM)
    v = (mn @ w_cv).reshape(B, N_MEM, X_HEADS, X_HEAD_DIM)
    return k, v


def cross_attn(x, mk, mv, w_cq, w_co):
    B, T, _ = x.shape
    q = (x @ w_cq).reshape(B, T, X_HEADS, X_HEAD_DIM).astype(jnp.float32)
    s = jnp.einsum('bthd,bmhd->bhtm', q, mk.astype(jnp.float32)) * (X_HEAD_DIM ** -0.5)
    p = jax.nn.softmax(s, axis=-1)
    o = jnp.einsum('bhtm,bmhd->bthd', p, mv.astype(jnp.float32)).reshape(B, T, D_MODEL)
    return o.astype(x.dtype) @ w_co


def layer(x, pos, conv_buf, C0, n0, m0, S0, mk_mem, mv_mem,
          w_in, b_gate, w_conv, b_conv, g_mix, g_mhead, g_rhead, w_out,
          g_xattn, w_cq, w_co, g_ffn, w_gate, w_up, w_down):
    f32 = jnp.float32
    B, T, _ = x.shape
    M, R, G0 = M_WIDTH, R_WIDTH, 4 * M_WIDTH
    u = rmsnorm(x, g_mix) @ w_in
    qk, conv_new = short_conv(u[..., :2 * M], conv_buf, w_conv, b_conv)
    mq = qk[..., :M].reshape(B, T, M_HEADS, M_HEAD_DIM).astype(f32)
    mk = qk[..., M:].reshape(B, T, M_HEADS, M_HEAD_DIM).astype(f32) * (M_HEAD_DIM ** -0.5)
    mv = u[..., 2 * M:3 * M].reshape(B, T, M_HEADS, M_HEAD_DIM).astype(f32)
    mo = u[..., 3 * M:4 * M].astype(f32)
    gates = u[..., G0:G0 + 2 * M_HEADS].astype(f32) + b_gate.astype(f32)
    ig = gates[..., :M_HEADS]
    lf = jax.nn.log_sigmoid(gates[..., M_HEADS:])
    hm, C, n, m = mlstm_scan(mq, mk, mv, ig, lf, C0.astype(f32), n0.astype(f32), m0.astype(f32))
    hm = head_rmsnorm(hm, g_mhead) * jax.nn.sigmoid(mo)
    r = u[..., G0 + 2 * M_HEADS:]
    rq = rope(r[..., :R].reshape(B, T, R_HEADS, R_HEAD_DIM).astype(f32), pos)
    rk = rope(r[..., R:2 * R].reshape(B, T, R_HEADS, R_HEAD_DIM).astype(f32), pos) * (R_HEAD_DIM ** -0.5)
    rv = r[..., 2 * R:3 * R].reshape(B, T, R_HEADS, R_HEAD_DIM).astype(f32)
    rg = r[..., 3 * R:].astype(f32)
    hr, S = retention_scan(rq, rk, rv, S0.astype(f32))
    hr = head_rmsnorm(hr, g_rhead) * jax.nn.silu(rg)
    x = x + jnp.concatenate([hm, hr], axis=-1).astype(x.dtype) @ w_out
    x = x + cross_attn(rmsnorm(x, g_xattn), mk_mem, mv_mem, w_cq, w_co)
    hf = rmsnorm(x, g_ffn)
    x = x + (jax.nn.silu(hf @ w_gate) * (hf @ w_up)) @ w_down
    return x, conv_new, C, n, m, S


def setup_inputs(seed: int = 0) -> dict:
    key = jax.random.key(seed)
    ks = jax.random.split(key, 32)
    f32 = jnp.float32
    nrm = lambda k, shape, s: jax.random.normal(k, shape, f32) * s
    gain = lambda k, shape: 1.0 + 0.01 * jax.random.normal(k, shape, f32)
    b_i = 0.01 * jax.random.normal(ks[10], (DEPTH, M_HEADS), f32)
    b_f = jnp.linspace(3.0, 6.0, M_HEADS, dtype=f32)[None, :] + 0.01 * jax.random.normal(ks[11], (DEPTH, M_HEADS), f32)
    return {
        "x_prompt": nrm(ks[0], (BATCH, SEQ, D_MODEL), 1.0),
        "x_sample": nrm(ks[1], (DEC_BATCH, DEC_SEQ, D_MODEL), 1.0),
        "cache_mem_k": nrm(ks[2], (DEPTH, DEC_BATCH, N_MEM, X_HEADS, X_HEAD_DIM), 1.0),
        "cache_mem_v": nrm(ks[3], (DEPTH, DEC_BATCH, N_MEM, X_HEADS, X_HEAD_DIM), 1.0),
        "state_mlstm_conv": nrm(ks[4], (DEPTH, DEC_BATCH, CONV_W - 1, 2 * M_WIDTH), 1.0),
        "state_mlstm_C": nrm(ks[5], (DEPTH, DEC_BATCH, M_HEADS, M_HEAD_DIM, M_HEAD_DIM), 0.1),
        "state_mlstm_n": nrm(ks[6], (DEPTH, DEC_BATCH, M_HEADS, M_HEAD_DIM), 0.1),
        "state_mlstm_m": nrm(ks[7], (DEPTH, DEC_BATCH, M_HEADS), 1.0),
        "state_ret_S": nrm(ks[8], (DEPTH, DEC_BATCH, R_HEADS, R_HEAD_DIM, R_HEAD_DIM), 0.5),
        "mem_prompt": nrm(ks[9], (BATCH, N_MEM, D_MODEL), 1.0),
        "w_in": nrm(ks[12], (DEPTH, D_MODEL, IN_COLS), D_MODEL ** -0.5),
        "b_gate": jnp.concatenate([b_i, b_f], axis=-1),
        "w_conv": nrm(ks[13], (DEPTH, CONV_W, 2 * M_WIDTH), CONV_W ** -0.5),
        "b_conv": nrm(ks[14], (DEPTH, 2 * M_WIDTH), 0.01),
        "g_mix": gain(ks[15], (DEPTH, D_MODEL)),
        "g_mhead": gain(ks[16], (DEPTH, M_WIDTH)),
        "g_rhead": gain(ks[17], (DEPTH, R_WIDTH)),
        "w_out": nrm(ks[18], (DEPTH, MIX_WIDTH, D_MODEL), MIX_WIDTH ** -0.5),
        "g_xattn": gain(ks[19], (DEPTH, D_MODEL)),
        "g_mem": gain(ks[20], (DEPTH, D_MODEL)),
        "w_ck": nrm(ks[21], (DEPTH, D_MODEL, D_MODEL), D_MODEL ** -0.5),
        "w_cv": nrm(ks[22], (DEPTH, D_MODEL, D_MODEL), D_MODEL ** -0.5),
        "w_cq": nrm(ks[23], (DEPTH, D_MODEL, D_MODEL), D_MODEL ** -0.5),
        "w_co": nrm(ks[24], (DEPTH, D_MODEL, D_MODEL), D_MODEL ** -0.5),
        "g_ffn": gain(ks[25], (DEPTH, D_MODEL)),
        "w_gate": nrm(ks[26], (DEPTH, D_MODEL, D_FF), D_MODEL ** -0.5),
        "w_up": nrm(ks[27], (DEPTH, D_MODEL, D_FF), D_MODEL ** -0.5),
        "w_down": nrm(ks[28], (DEPTH, D_FF, D_MODEL), D_FF ** -0.5),
        "g_final": gain(ks[29], (D_MODEL,)),
    }


def reference(x_prompt, x_sample, cache_mem_k, cache_mem_v, state_mlstm_conv, state_mlstm_C,
              state_mlstm_n, state_mlstm_m, state_ret_S, mem_prompt,
              w_in, b_gate, w_conv, b_conv, g_mix, g_mhead, g_rhead, w_out, g_xattn, g_mem,
              w_ck, w_cv, w_cq, w_co, g_ffn, w_gate, w_up, w_down, g_final):
    f32 = jnp.float32
    Bp, Tp, _ = x_prompt.shape
    Ts = x_sample.shape[1]
    pos_p = jnp.arange(Tp, dtype=f32)
    pos_s = PAST_LEN + jnp.arange(Ts, dtype=f32)
    h_p, h_s = x_prompt, x_sample
    mkp_l, mvp_l, convp_l, Cp_l, np_l, mp_l, Sp_l = [], [], [], [], [], [], []
    convs_l, Cs_l, ns_l, ms_l, Ss_l = [], [], [], [], []
    for l in range(DEPTH):
        lw = (w_in[l], b_gate[l], w_conv[l], b_conv[l], g_mix[l], g_mhead[l], g_rhead[l], w_out[l],
              g_xattn[l], w_cq[l], w_co[l], g_ffn[l], w_gate[l], w_up[l], w_down[l])
        mk_p, mv_p = mem_kv(mem_prompt, g_mem[l], w_ck[l], w_cv[l])
        h_p, conv_p, C_p, n_p, m_p, S_p = layer(
            h_p, pos_p,
            jnp.zeros((Bp, CONV_W - 1, 2 * M_WIDTH), h_p.dtype),
            jnp.zeros((Bp, M_HEADS, M_HEAD_DIM, M_HEAD_DIM), f32),
            jnp.zeros((Bp, M_HEADS, M_HEAD_DIM), f32),
            jnp.zeros((Bp, M_HEADS), f32),
            jnp.zeros((Bp, R_HEADS, R_HEAD_DIM, R_HEAD_DIM), f32),
            mk_p, mv_p, *lw)
        h_s, conv_s, C_s, n_s, m_s, S_s = layer(
            h_s, pos_s, state_mlstm_conv[l], state_mlstm_C[l], state_mlstm_n[l], state_mlstm_m[l],
            state_ret_S[l], cache_mem_k[l], cache_mem_v[l], *lw)
        mkp_l.append(mk_p); mvp_l.append(mv_p); convp_l.append(conv_p)
        Cp_l.append(C_p); np_l.append(n_p); mp_l.append(m_p); Sp_l.append(S_p)
        convs_l.append(conv_s); Cs_l.append(C_s); ns_l.append(n_s); ms_l.append(m_s); Ss_l.append(S_s)
    y_prompt = rmsnorm(h_p, g_final)
    y_sample = rmsnorm(h_s, g_final)
    return (y_prompt, y_sample,
            jnp.stack(mkp_l), jnp.stack(mvp_l), jnp.stack(convp_l), jnp.stack(Cp_l),
            jnp.stack(np_l), jnp.stack(mp_l), jnp.stack(Sp_l),
            jnp.stack(convs_l), jnp.stack(Cs_l), jnp.stack(ns_l), jnp.stack(ms_l), jnp.stack(Ss_l))
```

```python
import contextlib
import threading
import numpy as np
import concourse.bass as bass
import concourse.mybir as mybir

F32 = mybir.dt.float32
BF16 = mybir.dt.bfloat16
AF = mybir.ActivationFunctionType
ALU = mybir.AluOpType
AX = mybir.AxisListType

SEG = 30000


class Buf:
    def __init__(self, fw, t, name):
        self.fw = fw
        self.t = t
        self.name = name
        self.w = []
        self.r = []
        self.sem = None
        self.is_dram = False
        self.is_psum = False
        self.dcount = 0

    def __getitem__(self, idx):
        return self.t[idx]


class Interleaver:
    def __init__(self, ratio=1):
        self.turn = threading.Semaphore(0)
        self.back = threading.Semaphore(0)
        self.alive = False
        self.thread = None
        self.err = None
        self.ratio = ratio
        self.credit = 0

    def start(self, fn):
        def run():
            self.turn.acquire()
            try:
                fn()
            except BaseException as e:
                self.err = e
            finally:
                self.alive = False
                self.back.release()
        self.alive = True
        self.thread = threading.Thread(target=run)
        self.thread.start()

    def in_helper(self):
        return self.thread is not None and threading.current_thread() is self.thread

    def after_op(self):
        if self.in_helper():
            self.credit -= 1
            if self.credit <= 0:
                self.back.release()
                self.turn.acquire()
                self.credit = self.ratio
        elif self.alive:
            self.turn.release()
            self.back.acquire()
            if self.err is not None:
                raise self.err

    def helper_wait(self, cond):
        while not cond():
            self.back.release()
            self.turn.acquire()
        self.credit = self.ratio

    def main_wait(self, cond):
        while self.alive and not cond():
            self.turn.release()
            self.back.acquire()
            if self.err is not None:
                raise self.err

    def drain(self):
        while self.alive:
            self.turn.release()
            self.back.acquire()
        if self.thread is not None:
            self.thread.join()
        if self.err is not None:
            raise self.err
        self.thread = None


class FW:
    def __init__(self, nc, stack):
        self.nc = nc
        self.stack = stack
        self.engs = {"pe": nc.tensor, "act": nc.scalar, "dve": nc.vector, "pool": nc.gpsimd,
                     "sp": nc.sync}
        self.count = {k: 0 for k in self.engs}
        self.esems = {k: [] for k in self.engs}
        self.known = {k: {} for k in self.engs}
        self.sems = {}
        self.nsem = 0
        self.bufs = []
        self.il = None

    def new_sem(self, name):
        s = self.stack.enter_context(self.nc.semaphore(name))
        self.nsem += 1
        return s

    def sb(self, name, shape, dt=F32):
        t = self.stack.enter_context(self.nc.sbuf_tensor(name, list(shape), dt))
        b = Buf(self, t, name)
        self.bufs.append(b)
        return b

    def ps(self, name, shape, dt=F32):
        t = self.stack.enter_context(self.nc.psum_tensor(name, list(shape), dt))
        b = Buf(self, t, name)
        b.is_psum = True
        self.bufs.append(b)
        return b

    def dram(self, name, shape, dt=F32, kind="Internal"):
        t = self.nc.dram_tensor(name, list(shape), dt, kind=kind)
        b = Buf(self, t, name)
        b.is_dram = True
        self.bufs.append(b)
        return b

    def view(self, buf, name=None):
        b = Buf(self, buf.t, name or buf.name + "_v")
        self.bufs.append(b)
        return b

    def _esem(self, eng, seq):
        si = (seq - 1) // SEG
        key = (eng, si)
        if key not in self.sems:
            self.sems[key] = self.new_sem(f"e_{eng}_{si}")
        return key, (seq - 1) % SEG + 1

    def _wait(self, eng, ev):
        key, val = ev
        kn = self.known[eng]
        if kn.get(key, 0) >= val:
            return
        if isinstance(key, tuple) and key[0] in self.engs:
            for (k2, v2) in list(kn.items()):
                if isinstance(k2, tuple) and k2[0] == key[0] and k2[1] > key[1]:
                    return
        self.engs[eng].wait_ge(self.sems[key], val)
        kn[key] = val

    def _deps(self, eng, reads, writes):
        evs = []
        for b in reads:
            evs += b.w
            if b.is_psum:
                evs += [e for e in b.r if not (isinstance(e[0], tuple) and e[0][0] == eng)]
        for b in writes:
            evs += b.w
            evs += b.r
        for ev in evs:
            if eng == "pe" and isinstance(ev[0], tuple) and ev[0][0] == "pe":
                continue
            self._wait(eng, ev)

    def _record(self, ev, reads, writes):
        for b in writes:
            b.w = [ev]
            b.r = []
        for b in reads:
            if b in writes:
                continue
            b.r = [e for e in b.r if e[0] != ev[0]] + [ev]

    def op(self, eng, fn, reads=(), writes=(), inc=True):
        reads = [b for b in reads if b is not None]
        writes = [b for b in writes if b is not None]
        self._deps(eng, reads, writes)
        seq = self.count[eng] + 1
        ev = self._esem(eng, seq)
        ins = fn(self.engs[eng])
        if inc:
            ins.then_inc(self.sems[ev[0]], 1)
            self.count[eng] = seq
        self._record(ev, reads, writes)
        if self.il is not None and not (eng == "pe" and not inc):
            self.il.after_op()
        return ins

    def dma(self, q, out_buf, out_ap, in_buf, in_ap, sem_buf=None, join=False, **kw):
        sb = sem_buf
        if sb is None:
            sb = in_buf if (out_buf.is_dram and not in_buf.is_dram) else out_buf
        kind = "sw" if q == "pool" else "hw"
        if sb.sem is None:
            sb.sem = {}
            sb.dcount = {}
        if kind not in sb.sem:
            sb.sem[kind] = self.new_sem("d_" + kind + "_" + sb.name)
            sb.dcount[kind] = 0
            self.sems[("dma", id(sb), kind)] = sb.sem[kind]
        key = ("dma", id(sb), kind)
        evs = list(in_buf.w) + list(out_buf.r)
        if join:
            evs += [e for e in out_buf.w if e[0] != key]
        else:
            evs += list(out_buf.w)
        for ev in evs:
            self._wait(q, ev)
        sb.dcount[kind] += 16
        ev = (key, sb.dcount[kind])
        self.engs[q].dma_start(out=out_ap, in_=in_ap, **kw).then_inc(sb.sem[kind], 16)
        if join:
            out_buf.w = [e for e in out_buf.w if e[0] != key] + [ev]
        else:
            out_buf.w = [ev]
            out_buf.r = []
        in_buf.r = [e for e in in_buf.r if e[0] != key] + [ev]
        if self.il is not None:
            self.il.after_op()
        return ev

    def wait_all(self, eng, bufs):
        for b in bufs:
            for ev in b.w + b.r:
                self._wait(eng, ev)

from concourse.bass_utils import run_bass_kernel_spmd

T = 2048
NS = 16
D = 1024
NTOK = T + NS
DFF = 2816
KS = 128 ** -0.5
EPS = 1e-6
G_MIX, G_XATTN, G_FFN, G_MEM = 0, 1, 2, 3


import os


class _Stop(Exception):
    pass


class Rot:
    def __init__(self, fw, name, shape, dt, n):
        self.b = [fw.sb(f"{name}{i}", shape, dt) for i in range(n)]
        self.i = 0

    def __call__(self):
        b = self.b[self.i % len(self.b)]
        self.i += 1
        return b


def _scope_patch(FWc):
    def push_scope(self):
        self.gstack = getattr(self, "gstack", self.stack)
        s = contextlib.ExitStack()
        s.__enter__()
        self._scopes = getattr(self, "_scopes", []) + [(s, self.stack)]
        self.stack = s

    def pop_scope(self):
        self.barrier()
        s, prev = self._scopes.pop()
        s.__exit__(None, None, None)
        self.stack = prev

    def new_sem(self, name):
        g = getattr(self, "gstack", self.stack)
        s = g.enter_context(self.nc.semaphore(name))
        self.nsem += 1
        return s

    def barrier(self):
        evs = []
        for eng in ("pe", "act", "dve", "pool"):
            if self.count[eng] > 0:
                evs.append(self._esem(eng, self.count[eng]))
        for b in self.bufs:
            if b.sem is not None:
                for kind, cntv in b.dcount.items():
                    if cntv > 0:
                        evs.append((("dma", id(b), kind), cntv))
        for eng in ("pe", "act", "dve", "pool", "sp"):
            for ev in evs:
                if eng == "pe" and ev[0][0] == "pe":
                    continue
                self._wait(eng, ev)
        self.bufs = [b for b in self.bufs]

    FWc.push_scope = push_scope
    FWc.pop_scope = pop_scope
    FWc.new_sem = new_sem
    FWc.barrier = barrier


_scope_patch(FW)


def build_nc():
    nc = bass.Bass("TRN2", target_bir_lowering=False)
    st = contextlib.ExitStack()
    with st:
        fw = FW(nc, st)
        fw.gstack = st
        nc.allow_low_precision("bf16 matmul operands with fp32 PSUM accumulation")
        try:
            _build(nc, fw)
        except _Stop:
            for o_ in fw.outs:
                fw.wait_all("sp", [o_])
            fw.barrier()
            while getattr(fw, "_scopes", []):
                s_, prev = fw._scopes.pop()
                s_.__exit__(None, None, None)
                fw.stack = prev
    return nc


def _build(nc, fw):
    I = lambda n, s: fw.dram(n, s, F32, kind="ExternalInput")
    O = lambda n, s: fw.dram(n, s, F32, kind="ExternalOutput")
    x_p = I("x_p", [T, D]); x_s = I("x_s", [NS, D]); mem_p = I("mem_p", [256, D])
    ck_s = I("ck_s", [NS, 256, D]); cv_s = I("cv_s", [NS, 256, D])
    conv_s = I("conv_s", [NS, 3, D]); C_s = I("C_s", [NS, 4, 128, 128]); n_s = I("n_s", [NS, 512])
    m_s = I("m_s", [NS, 4]); S_s = I("S_s", [NS, 4, 128, 128])
    w_in = I("w_in", [D, 4104]); w_out = I("w_out", [D, D]); w_ck = I("w_ck", [D, D]); w_cv = I("w_cv", [D, D])
    w_cq = I("w_cq", [D, D]); w_co = I("w_co", [D, D]); w_gate = I("w_gate", [D, DFF]); w_up = I("w_up", [D, DFF])
    w_down = I("w_down", [DFF, D])
    c_ident = I("c_ident", [128, 128]); c_maskb = I("c_maskb", [128, 128]); c_decayT = I("c_decayT", [128, 512])
    c_qdec = I("c_qdec", [128, 512]); c_kdec = I("c_kdec", [128, 4]); c_cdec = I("c_cdec", [128, 4]); c_gam = I("c_gam", [128, 4])
    c_sel4 = I("c_sel4", [4, 512]); c_sel16 = I("c_sel16", [16, 2048])
    c_cos = I("c_cos", [128, 17, 64]); c_sin = I("c_sin", [128, 17, 64])
    gcols_d = I("gcols", [128, 4, 8]); gmh_d = I("gmh_bc", [128, 512]); grh_d = I("grh_bc", [128, 512]); gfin_d = I("gfin_bc", [128, D])
    bgcol_d = I("bgate_col", [4, 2]); bgbc_d = I("bgate_bc", [16, 8]); wccol_d = I("wconv_col", [128, 8, 4]); bccol_d = I("bconv_col", [128, 8])
    wcbc_d = I("wconv_bc", [16, 4, D]); bcbc_d = I("bconv_bc", [16, D])

    y_p = O("y_p", [T, D]); y_s = O("y_s", [NS, D]); mk_o = O("mk_o", [256, D]); mv_o = O("mv_o", [256, D])
    conv_po = O("conv_po", [3, D]); C_po = O("C_po", [4, 128, 128]); n_po = O("n_po", [4, 128]); m_po = O("m_po", [4, 1])
    S_po = O("S_po", [4, 128, 128]); conv_so = O("conv_so", [NS, 3, D]); C_so = O("C_so", [NS, 4, 128, 128])
    n_so = O("n_so", [NS, 512]); m_so = O("m_so", [NS, 4]); S_so = O("S_so", [NS, 4, 128, 128])
    outs = [y_p, y_s, mk_o, mv_o, conv_po, C_po, n_po, m_po, S_po, conv_so, C_so, n_so, m_so, S_so]
    fw.outs = outs

    def stop(tag):
        if os.environ.get("MK_STOP") == tag:
            raise _Stop()

    wg_bf = fw.dram("wg_bf", [128, 8, DFF], BF16); wu_bf = fw.dram("wu_bf", [128, 8, DFF], BF16)
    wd_bf = fw.dram("wd_bf", [128, 22, D], BF16)
    wout_bf = fw.dram("wout_bf", [128, 8, D], BF16); wcq_bf = fw.dram("wcq_bf", [128, 8, D], BF16); wco_bf = fw.dram("wco_bf", [128, 8, D], BF16)

    def cload(name, d, shape, dt=F32, q="sp"):
        b = fw.sb(name, shape, F32)
        fw.dma(q, b, b[:], d, d[:])
        return b
    identf = cload("identf", c_ident, [128, 128]); maskb = cload("maskb", c_maskb, [128, 128])
    decayT = cload("decayT", c_decayT, [128, 512]); qdec = cload("qdec", c_qdec, [128, 512])
    kdec = cload("kdec", c_kdec, [128, 4]); cdec = cload("cdec", c_cdec, [128, 4]); gam = cload("gam", c_gam, [128, 4])
    cosT = cload("cosT", c_cos, [128, 17, 64]); sinT = cload("sinT", c_sin, [128, 17, 64])
    gcols = cload("gcols_s", gcols_d, [128, 4, 8]); gmh = cload("gmh", gmh_d, [128, 512]); grh = cload("grh", grh_d, [128, 512])
    gfin = cload("gfin", gfin_d, [128, D]); bgcol = cload("bgcol", bgcol_d, [4, 2]); bgbc = cload("bgbc", bgbc_d, [16, 8])
    wccol = cload("wccol", wccol_d, [128, 8, 4]); bccol = cload("bccol", bccol_d, [128, 8])
    identb = fw.sb("identb", [128, 128], BF16)
    fw.op("dve", lambda e: e.tensor_copy(identb[:], identf[:]), [identf], [identb])
    onesb = fw.sb("onesb", [128, 128], BF16); onesf = fw.sb("onesf", [128, 128])
    fw.op("dve", lambda e: e.memset(onesb[:], 1.0), [], [onesb])
    fw.op("dve", lambda e: e.memset(onesf[:], 1.0), [], [onesf])
    kmemT = fw.sb("kmemT", [128, 8, 256], BF16); vmem = fw.sb("vmem", [128, 2, D], BF16)
    hmrT_d = fw.dram("hmrT_d", [128, 8, NTOK], BF16)
    hstage = Rot(fw, "hstage", [128, 4, 128], BF16, 2)

    pf = [fw.ps(f"pf{i}", [128, 512]) for i in range(7)]
    pb = [fw.ps(f"pb{i}", [128, 1024], BF16) for i in range(1)]
    cnt = {"pf": 0, "pb": 0, "e": 0}

    pools = {"all": [0, 1, 2, 3, 4, 5], "front": [0, 1, 2], "back": [3, 4, 5, 6], "smp": [5]}

    def PF(pool="all"):
        if fw.il is not None and fw.il.in_helper():
            pool = "smp"
        k_ = "pf_" + pool
        cnt[k_] = cnt.get(k_, 0) + 1
        lst = pools[pool]
        return pf[lst[cnt[k_] % len(lst)]]

    def PB():
        cnt["pb"] += 1
        return pb[0]

    pl = pf[6]
    xtile = Rot(fw, "xt", [128, D], F32, 2)
    xnb = Rot(fw, "xnb", [128, D], BF16, 3)
    junk = Rot(fw, "junk", [128, D], BF16, 1)
    stat = Rot(fw, "stat", [128, 16], F32, 6)
    mhalf = fw.sb("mhalf", [128, 16])
    fw.op("pool", lambda e: e.memset(mhalf[:], -0.5), [], [mhalf])

    def mm(ob, oap, lb, lap, rb, rap, start, stop, fin):
        fw.op("pe", lambda e: e.matmul(oap, lap, rap, start=start, stop=stop), [lb, rb], [ob], inc=fin)

    def tr(ob, oap, ib, iap, R, fin, f32=False):
        idn = (identf if f32 else identb)
        fw.op("pe", lambda e: e.transpose(oap, iap, idn[:R, :R]), [ib, idn], [ob], inc=fin)

    def rows(tt):
        return 128 if tt < 16 else NS

    def rstd_of(ssq_ap, out_ap, sb_, R, n):
        w_ = ssq_ap.shape[1]
        fw.op("pool", lambda e: e.tensor_scalar(out_ap, ssq_ap, 1.0 / n, EPS, ALU.mult, ALU.add), [sb_], [sb_])
        fw.op("pool", lambda e: e.tensor_tensor(out_ap, out_ap, mhalf[:R, 0:w_], ALU.pow), [sb_, mhalf], [sb_])

    def sig_gate(R, src_ap, src_buf, e_buf, out_buf, mul_buf, with_x):
        fw.op("act", lambda e: e.activation(e_buf[:R, :], src_ap, AF.Exp, scale=-1.0), [src_buf], [e_buf])
        fw.op("act", lambda e: e.activation(e_buf[:R, :], e_buf[:R, :], AF.Ln, bias=1.0), [e_buf], [e_buf])
        fw.op("act", lambda e: e.activation(e_buf[:R, :], e_buf[:R, :], AF.Exp, scale=-1.0), [e_buf], [e_buf])
        if with_x:
            fw.op("dve", lambda e: e.tensor_tensor(e_buf[:R, :], e_buf[:R, :], src_ap, ALU.mult), [e_buf, src_buf], [e_buf])
        fw.op("pool", lambda e: e.tensor_tensor(out_buf[:R, :], e_buf[:R, :], mul_buf[:R, :], ALU.mult), [e_buf, mul_buf], [out_buf])

    def norm_T(src, R, gi, dst_buf, dst_ap, defer=False):
        sq = junk(); s_ = stat()
        fw.op("dve", lambda e: e.memset(s_[:R, 0:1], 0.0), [], [s_])
        fw.op("act", lambda e: e.activation(sq[:R, :], src[:R, :], AF.Square, accum_out=s_[:R, 0:1]), [src], [sq, s_])
        rstd_of(s_[:R, 0:1], s_[:R, 1:2], s_, R, D)
        xn = xnb()
        fw.op("dve", lambda e: e.tensor_scalar_mul(xn[:R, :], src[:R, :], s_[:R, 1:2]), [src, s_], [xn])

        def part2():
            p_ = PB()
            for kc in range(8):
                tr(p_, p_[:, kc * 128:kc * 128 + R], xn, xn[:R, kc * 128:(kc + 1) * 128], R, kc == 7)
            fw.op("dve", lambda e: e.tensor_tensor(dst_ap, p_[:].rearrange("p (k t) -> p k t", k=8)[:, :, :R],
                                                   gcols[:, gi, :].unsqueeze(2).to_broadcast([128, 8, R]), ALU.mult),
                  [p_, gcols], [dst_buf])
        if defer:
            return part2
        part2()
        return None

    cast_engs = ["pool", "dve", "act"]

    def cast(eng, ob, oap, ib, iap):
        if eng == "act":
            fw.op("act", lambda e: e.activation(oap, iap, AF.Copy), [ib], [ob])
        else:
            fw.op(eng, lambda e: e.tensor_copy(oap, iap), [ib], [ob])

    def wcast(dst, w, c0, c1, nsplit=4):
        n = c1 - c0
        step = max(1, 8 // nsplit)
        for k0 in range(0, 8, step):
            fw.dma("pool", dst, dst[:, k0:k0 + step, 0:n], w, w[k0 * 128:(k0 + step) * 128, c0:c1].rearrange("(k p) c -> p k c", p=128), join=True)

    bg_jobs = []
    for (w, dst) in ((w_out, wout_bf), (w_cq, wcq_bf), (w_co, wco_bf), (w_gate, wg_bf), (w_up, wu_bf)):
        for k0 in range(0, 8, 2):
            bg_jobs.append((dst, dst[:, k0:k0 + 2, :], w, w[k0 * 128:(k0 + 2) * 128, :].rearrange("(k p) c -> p k c", p=128)))
    for f0 in range(0, 22, 2):
        bg_jobs.append((wd_bf, wd_bf[:, f0:f0 + 2, :], w_down, w_down[f0 * 128:(f0 + 2) * 128, :].rearrange("(f p) c -> p f c", p=128)))

    def bg_step(n=1):
        for _ in range(n):
            if bg_jobs:
                d_, dap, w_, wap = bg_jobs.pop(0)
                fw.dma("pool", d_, dap, w_, wap, join=True)

    stop("c0")
    fw.push_scope()
    mnT = fw.sb("mnT", [128, 8, 256], BF16)
    wck = fw.sb("wck", [128, 8, D], BF16); wcv = fw.sb("wcv", [128, 8, D], BF16)
    for t2 in range(2):
        xt = xtile()
        fw.dma("sp", xt, xt[:, :], mem_p, mem_p[t2 * 128:(t2 + 1) * 128, :])
        norm_T(xt, 128, G_MEM, mnT, mnT[:, :, t2 * 128:(t2 + 1) * 128])
    stop("p0a")
    wcast(wck, w_ck, 0, D)
    wcast(wcv, w_cv, 0, D)
    stop("p0b")
    for (w, outd, isv) in ((wck, mk_o, False), (wcv, mv_o, True)):
        for t2 in range(2):
            ot = xtile()
            for half in range(2):
                p_ = PF()
                for kc in range(8):
                    mm(p_, p_[:, :512], mnT, mnT[:, kc, t2 * 128:(t2 + 1) * 128], w, w[:, kc, half * 512:(half + 1) * 512], kc == 0, kc == 7, kc == 7)
                fw.op("act", lambda e, p_=p_, half=half, ot=ot: e.activation(ot[:, half * 512:(half + 1) * 512], p_[:, :512], AF.Copy), [p_], [ot])
                if isv:
                    fw.op("dve", lambda e, p_=p_, half=half, t2=t2: e.tensor_copy(vmem[:, t2, half * 512:(half + 1) * 512], p_[:, :512]), [p_], [vmem])
            fw.dma("sp", outd, outd[t2 * 128:(t2 + 1) * 128, :], ot, ot[:, :], join=True)
    stop("p0c")
    for ct in range(8):
        p_ = PF()
        for kc in range(8):
            mm(p_, p_[:, :256], wck, wck[:, kc, ct * 128:(ct + 1) * 128], mnT, mnT[:, kc, :], kc == 0, kc == 7, kc == 7)
        fw.op("act", lambda e, p_=p_, ct=ct: e.activation(kmemT[:, ct, :], p_[:, :256], AF.Copy), [p_], [kmemT])
    fw.pop_scope()

    stop("p0")
    fw.push_scope()
    xnT = fw.sb("xnT", [128, 8, NTOK], BF16)
    xnT_v = [fw.view(xnT, f"xnT{i}") for i in range(17)]
    pend1 = None
    for tt in range(17):
        R = rows(tt)
        xt = xtile()
        if tt < 16:
            fw.dma("sp", xt, xt[:R, :], x_p, x_p[tt * 128:(tt + 1) * 128, :])
        else:
            fw.dma("sp", xt, xt[:R, :], x_s, x_s[:, :])
        if pend1 is not None:
            pend1()
        pend1 = norm_T(xt, R, G_MIX, xnT_v[tt], xnT[:, :, tt * 128:tt * 128 + R], defer=True)
    pend1()
    XV = lambda mt: xnT_v[mt * 4:(mt + 1) * 4]
    stop("p1a")

    fw.push_scope()
    wgt = fw.sb("wgt", [128, 8, 8], BF16)
    wcast(wgt, w_in, 2048, 2056, nsplit=1)

    GT = fw.sb("GT", [128, 16, 68]); GT2 = fw.sb("GT2", [128, 16, 4])
    winA = fw.sb("winA", [128, 8, 2048], BF16)
    wcast(winA, w_in, 0, 2048)
    fw.push_scope()
    gA = fw.sb("gA", [4, T]); gF = fw.sb("gF", [4, T]); Bn = fw.sb("Bn", [4, T]); Mh = fw.sb("Mh", [4, T])
    ones4 = fw.sb("ones4", [4, T]); TP = fw.sb("TP", [68, T]); msm = fw.sb("msm", [4, 64]); TP2 = fw.sb("TP2", [4, T])
    fw.op("pool", lambda e: e.memset(ones4[:], 1.0), [], [ones4])
    fw.op("pool", lambda e: e.memset(TP[:], 0.0), [], [TP])
    for mt in range(4):
        for gi_, dstb in ((0, gA), (1, gF)):
            p_ = PF()
            for kc in range(8):
                fw.op("pe", lambda e, p_=p_, kc=kc, gi_=gi_, mt=mt: e.matmul(p_[0:4, :512], wgt[:, kc, 4 * gi_:4 + 4 * gi_], xnT[:, kc, mt * 512:(mt + 1) * 512], start=(kc == 0), stop=(kc == 7)),
                      [wgt] + XV(mt), [p_], inc=(kc == 7))
            fw.op("act", lambda e, p_=p_, dstb=dstb, gi_=gi_, mt=mt: e.activation(dstb[:, mt * 512:(mt + 1) * 512], p_[0:4, :512], AF.Identity, bias=bgcol[:, gi_:gi_ + 1]), [p_, bgcol], [dstb])
    fw.op("act", lambda e: e.activation(gF[:], gF[:], AF.Exp, scale=-1.0), [gF], [gF])
    fw.op("act", lambda e: e.activation(gF[:], gF[:], AF.Ln, bias=1.0), [gF], [gF])
    fw.op("dve", lambda e: e.tensor_tensor_scan(Bn[:], ones4[:], gF[:], 0.0, ALU.mult, ALU.add), [ones4, gF], [Bn])
    fw.op("dve", lambda e: e.tensor_tensor(gA[:], gA[:], Bn[:], ALU.add), [gA, Bn], [gA])
    fw.op("dve", lambda e: e.tensor_tensor_scan(Mh[:], ones4[:], gA[:], 0.0, ALU.mult, ALU.max), [ones4, gA], [Mh])
    Mh3 = Mh[:].rearrange("p (c l) -> p c l", l=128)
    fw.op("dve", lambda e: e.memset(msm[:], 0.0), [], [msm])
    fw.op("dve", lambda e: e.tensor_copy(msm[:, 1:16], Mh3[:, 0:15, 127]), [Mh], [msm])
    fw.op("dve", lambda e: e.tensor_copy(msm[:, 16:32], Mh3[:, :, 127]), [Mh], [msm])
    fw.op("dve", lambda e: e.tensor_tensor(msm[:, 48:64], msm[:, 0:16], msm[:, 16:32], ALU.subtract), [msm], [msm])
    fw.op("act", lambda e: e.activation(msm[:, 48:64], msm[:, 48:64], AF.Exp), [msm], [msm])
    fw.op("dve", lambda e: e.tensor_copy(TP[0:4, :].rearrange("p (c l) -> p c l", l=128), msm[:, 48:64].unsqueeze(2).to_broadcast([4, 16, 128])), [msm], [TP])
    fw.op("dve", lambda e: e.tensor_tensor(gF[:].rearrange("p (c l) -> p c l", l=128), gA[:].rearrange("p (c l) -> p c l", l=128),
                                           msm[:, 16:32].unsqueeze(2).to_broadcast([4, 16, 128]), ALU.subtract), [gA, msm], [gF])
    fw.op("act", lambda e: e.activation(gF[:], gF[:], AF.Exp), [gF], [gF])
    fw.op("dve", lambda e: e.tensor_copy(TP[32:36, :], gF[:]), [gF], [TP])
    fw.op("dve", lambda e: e.tensor_tensor(gF[:].rearrange("p (c l) -> p c l", l=128), Mh3, msm[:, 16:32].unsqueeze(2).to_broadcast([4, 16, 128]), ALU.subtract), [Mh, msm], [gF])
    fw.op("act", lambda e: e.activation(TP2[:], gF[:], AF.Exp, scale=-1.0), [gF], [TP2])
    fw.op("dve", lambda e: e.tensor_tensor(gF[:], Bn[:], Mh[:], ALU.subtract), [Bn, Mh], [gF])
    fw.op("dve", lambda e: e.tensor_scalar_mul(msm[:, 32:33], gF[:, T - 1:T], -1.0), [gF], [msm])
    fw.dma("sp", m_po, m_po[:, :], msm, msm[:, 32:33])
    fw.op("act", lambda e: e.activation(gF[:], gF[:], AF.Exp), [gF], [gF])
    fw.op("dve", lambda e: e.tensor_copy(TP[64:68, :], gF[:]), [gF], [TP])
    for c in range(16):
        p_ = PF()
        tr(p_, p_[:, 64:68], TP2, TP2[0:4, c * 128:(c + 1) * 128], 4, True, f32=True)
        fw.op("act", lambda e, p_=p_, c=c: e.activation(GT2[:, c, :], p_[:, 64:68], AF.Copy), [p_], [GT2])
    for c in range(16):
        p_ = PF()
        tr(p_, p_[:, 0:68], TP, TP[0:68, c * 128:(c + 1) * 128], 68, True, f32=True)
        fw.op("act", lambda e, p_=p_, c=c: e.activation(GT[:, c, :], p_[:, 0:68], AF.Copy), [p_], [GT])
    fw.pop_scope()

    stop("p1b")
    fw.push_scope()
    pcj = Rot(fw, "pcj", [128, 515], F32, 2); halo = fw.sb("halo", [128, 8, 3]); acc = Rot(fw, "acc", [128, 512], F32, 2); sgk = Rot(fw, "sgk", [128, 512], F32, 1)
    fw.op("pool", lambda e: e.memset(halo[:, :, :], 0.0), [], [halo])
    qTr = Rot(fw, "qT", [128, 4, 512], BF16, 2); kTr = Rot(fw, "kT", [128, 4, 512], BF16, 2)
    vext = Rot(fw, "vext", [128, 4, 129], BF16, 2)
    for b_ in vext.b:
        fw.op("pool", lambda e, b_=b_: e.memset(b_[:, :, 128:129], 1.0), [], [b_])
    gsig = Rot(fw, "gsig", [128, 512], F32, 2); sgo = Rot(fw, "sgo", [128, 512], F32, 1)
    nm = Rot(fw, "nm", [128, 512], F32, 1); DTt = Rot(fw, "DTt", [128, 512], F32, 1)
    wts = Rot(fw, "wts", [128, 512], BF16, 2); wbc = Rot(fw, "wbc", [128, 512], F32, 2)
    qw = Rot(fw, "qw", [128, 4, 128], BF16, 2); vw = Rot(fw, "vw", [128, 4, 129], BF16, 2)
    ktok = Rot(fw, "ktok", [128, 512], BF16, 2); hmt = Rot(fw, "hmt", [128, 512], BF16, 3)
    CT = fw.sb("CT", [128, 4, 129]); CTb = fw.sb("CTb", [128, 4, 129], BF16); mask01 = fw.sb("mask01", [128, 128])
    fw.op("pool", lambda e: e.tensor_scalar(mask01[:, :], maskb[:, :], 1.0 / 30000.0, 1.0, ALU.mult, ALU.add), [maskb], [mask01])
    fw.op("pool", lambda e: e.memset(CT[:], 0.0), [], [CT])
    fw.op("pool", lambda e: e.memset(CTb[:], 0.0), [], [CTb])
    ej = Rot(fw, "ej", [128, 128], BF16, 2)

    def epilogue(R, nums, num_bufs, den_ap, den_buf, emt_ap, emt_buf, gs, hm, f_ap=None, f_buf=None):
        s_ = stat()
        fw.op("dve", lambda e: e.memset(s_[:R, 0:16], 0.0), [], [s_])
        for h in range(4):
            j_ = ej()
            fw.op("act", lambda e, h=h, j_=j_: e.activation(j_[:R, :], nums[h], AF.Square, accum_out=s_[:R, h:h + 1]), num_bufs, [j_, s_])
        if den_ap is not None:
            fw.op("dve", lambda e: e.tensor_scalar_mul(s_[:R, 8:12], den_ap, -1.0), [den_buf, s_], [s_])
            fw.op("dve", lambda e: e.tensor_tensor(s_[:R, 4:8], s_[:R, 8:12], den_ap, ALU.max), [den_buf, s_], [s_])
            if f_ap is not None:
                fw.op("dve", lambda e: e.tensor_tensor(s_[:R, 4:8], s_[:R, 4:8], f_ap, ALU.mult), [f_buf, s_], [s_])
            fw.op("dve", lambda e: e.tensor_tensor(s_[:R, 4:8], s_[:R, 4:8], emt_ap, ALU.max), [emt_buf, s_], [s_])
            fw.op("dve", lambda e: e.reciprocal(s_[:R, 4:8], s_[:R, 4:8]), [s_], [s_])
            if f_ap is not None:
                fw.op("dve", lambda e: e.tensor_tensor(s_[:R, 4:8], s_[:R, 4:8], f_ap, ALU.mult), [f_buf, s_], [s_])
            fw.op("dve", lambda e: e.tensor_tensor(s_[:R, 8:12], s_[:R, 4:8], s_[:R, 4:8], ALU.mult), [s_], [s_])
            fw.op("dve", lambda e: e.tensor_tensor(s_[:R, 0:4], s_[:R, 0:4], s_[:R, 8:12], ALU.mult), [s_], [s_])
        rstd_of(s_[:R, 0:4], s_[:R, 12:16], s_, R, 128)
        if den_ap is not None:
            fw.op("dve", lambda e: e.tensor_tensor(s_[:R, 12:16], s_[:R, 12:16], s_[:R, 4:8], ALU.mult), [s_], [s_])
        for h in range(4):
            fw.op("dve", lambda e, h=h: e.scalar_tensor_tensor(hm[:R, h * 128:(h + 1) * 128], nums[h], s_[:R, 12 + h:13 + h], gs[:R, h * 128:(h + 1) * 128], ALU.mult, ALU.mult),
                  num_bufs + [s_, gs], [hm])

    def hm_to_T(hm, R, tt, base):
        p_ = PB()
        for h in range(4):
            tr(p_, p_[:, h * 128:h * 128 + R], hm, hm[:R, h * 128:(h + 1) * 128], R, h == 3)
        hs_ = hstage()
        fw.op("act", lambda e: e.activation(hs_[:, :, :R], p_[:, 0:512].rearrange("p (h t) -> p h t", h=4)[:, :, :R], AF.Copy), [p_], [hs_])
        fw.dma("sp", hmrT_d, hmrT_d[:, base:base + 4, tt * 128:tt * 128 + R], hs_, hs_[:, :, :R], join=True)

    def conv_j(mt, j, qT, kT):
        msl = slice(mt * 512, (mt + 1) * 512)
        p_ = PF("front"); pc = pcj()
        for kc in range(8):
            fw.op("pe", lambda e, kc=kc: e.matmul(p_[:, :512], winA[:, kc, j * 128:(j + 1) * 128], xnT[:, kc, msl], start=(kc == 0), stop=(kc == 7)),
                  [winA] + XV(mt), [p_], inc=(kc == 7))
        fw.op("pool", lambda e: e.tensor_copy(pc[:, 0:3], halo[:, j, :]), [halo], [pc])
        fw.op("act", lambda e: e.activation(pc[:, 3:515], p_[:, :512], AF.Copy), [p_], [pc])
        fw.op("pool", lambda e: e.tensor_copy(halo[:, j, :], pc[:, 512:515]), [pc], [halo])
        a_ = acc()
        fw.op("dve", lambda e: e.tensor_scalar(a_[:, :], pc[:, 0:512], wccol[:, j, 0:1], bccol[:, j:j + 1], ALU.mult, ALU.add), [pc, wccol, bccol], [a_])
        for k in range(1, 4):
            fw.op("dve", lambda e, k=k: e.scalar_tensor_tensor(a_[:, :], pc[:, k:k + 512], wccol[:, j, k:k + 1], a_[:, :], ALU.mult, ALU.add), [pc, wccol, a_], [a_])
        g_ = sgk()
        fw.op("act", lambda e: e.activation(g_[:, :], a_[:, :], AF.Exp, scale=-1.0), [a_], [g_])
        fw.op("act", lambda e: e.activation(g_[:, :], g_[:, :], AF.Ln, bias=1.0), [g_], [g_])
        fw.op("act", lambda e: e.activation(g_[:, :], g_[:, :], AF.Exp, scale=-1.0), [g_], [g_])
        if j < 4:
            fw.op("pool", lambda e: e.tensor_tensor(qT[:, j, :], a_[:, :], g_[:, :], ALU.mult), [a_, g_], [qT])
        else:
            fw.op("dve", lambda e: e.scalar_tensor_tensor(kT[:, j - 4, :], a_[:, :], KS, g_[:, :], ALU.mult, ALU.mult), [a_, g_], [kT])
        if mt == 3:
            fw.dma("sp", conv_po, conv_po[:, j * 128:(j + 1) * 128].rearrange("t c -> c t"), halo, halo[:, j, :], join=True, allow_slow_non_contiguous=True)

    def front(c, qT, kT):
        cc = c % 4
        csl = slice(cc * 128, (cc + 1) * 128)
        tsl = slice(c * 128, (c + 1) * 128)
        xv = [xnT_v[c]]
        ve = vext(); p_ = PF("front")
        for kc in range(8):
            fw.op("pe", lambda e, p_=p_, kc=kc: e.matmul(p_[:, :512], xnT[:, kc, tsl], winA[:, kc, 1024:1536], start=(kc == 0), stop=(kc == 7)), [winA] + xv, [p_], inc=(kc == 7))
        fw.op("act", lambda e: e.activation(ve[:, :, 0:128], p_[:, :512].rearrange("p (h v) -> p h v", h=4), AF.Copy), [p_], [ve])
        p2_ = PF("front"); so = sgo(); gs = gsig()
        for kc in range(8):
            fw.op("pe", lambda e, kc=kc: e.matmul(p2_[:, :512], xnT[:, kc, tsl], winA[:, kc, 1536:2048], start=(kc == 0), stop=(kc == 7)), [winA] + xv, [p2_], inc=(kc == 7))
        sig_gate(128, p2_[:, :512], p2_, so, gs, gmh, False)
        ps_s = PF("front")
        for h in range(4):
            mm(ps_s, ps_s[:, h * 128:(h + 1) * 128], kT, kT[:, h, csl], qT, qT[:, h, csl], True, True, h == 3)
        w_ = wts()
        fw.op("dve", lambda e: e.tensor_tensor(w_[:, :].rearrange("p (h t) -> p h t", h=4), ps_s[:, :512].rearrange("p (h t) -> p h t", h=4),
                                               mask01[:, :].unsqueeze(1).to_broadcast([128, 4, 128]), ALU.mult), [ps_s, mask01], [w_])
        v_ = vw(); kt = ktok()
        fw.op("pool", lambda e: e.tensor_tensor(v_[:, :, :], ve[:, :, :], GT[:, c, 32:36].unsqueeze(2).to_broadcast([128, 4, 129]), ALU.mult), [ve, GT], [v_])
        p2 = PB()
        for h in range(4):
            tr(p2, p2[:, h * 128:(h + 1) * 128], kT, kT[:, h, csl], 128, h == 3)
        fw.op("act", lambda e: e.activation(kt[:, :], p2[:, 0:512], AF.Copy), [p2], [kt])
        return dict(gs=gs, w_=w_, v_=v_, kt=kt, qT=qT, csl=csl)

    def back_a(c, F):
        gs, w_, v_, kt, qT, csl = F["gs"], F["w_"], F["v_"], F["kt"], F["qT"], F["csl"]
        po = [PF("back"), PF("back")]
        for h in range(4):
            pb_ = po[h // 2]; o0 = (h % 2) * 129
            mm(pb_, pb_[:, o0:o0 + 129], w_, w_[:, h * 128:(h + 1) * 128], v_, v_[:, h, :], True, False, False)
            mm(pb_, pb_[:, o0:o0 + 129], qT, qT[:, h, csl], CTb, CTb[:, h, :], False, True, h % 2 == 1)
        pcs = [PF("back"), PF("back")]
        for h in range(4):
            pb_ = pcs[h // 2]; o0 = (h % 2) * 129
            mm(pb_, pb_[:, o0:o0 + 129], kt, kt[:, h * 128:(h + 1) * 128], v_, v_[:, h, :], True, True, h % 2 == 1)
        for h in range(4):
            pb_ = pcs[h // 2]; o0 = (h % 2) * 129
            fw.op("dve", lambda e, h=h, pb_=pb_, o0=o0: e.scalar_tensor_tensor(CT[:, h, :], CT[:, h, :], GT[:, c, h:h + 1], pb_[:, o0:o0 + 129], ALU.mult, ALU.add),
                  [CT, GT, pb_], [CT])
        if c + 1 < 16:
            fw.op("dve", lambda e: e.tensor_tensor(CTb[:, :, :], CT[:, :, :], GT[:, c + 1, 0:4].unsqueeze(2).to_broadcast([128, 4, 129]), ALU.mult), [CT, GT], [CTb])
        dn = stat()
        for i2 in range(2):
            fw.op("dve", lambda e, i2=i2: e.tensor_copy(dn[:, 2 * i2:2 * i2 + 2], po[i2][:, 0:258].rearrange("p (h v) -> p h v", h=2)[:, :, 128]), [po[i2]], [dn])
        nums = [po[h // 2][:, (h % 2) * 129:(h % 2) * 129 + 128] for h in range(4)]
        hm = hmt()
        epilogue(128, nums, po, dn[:, 0:4], dn, GT[:, c, 64:68], GT, gs, hm, f_ap=GT2[:, c, 0:4], f_buf=GT2)
        bg_step(1)
        return hm

    Fq = {}; Hq = {}
    qk_bufs = [(qTr(), kTr()), (qTr(), kTr())]
    prog = {"main": -1, "conv": -1}
    for j in range(8):
        conv_j(0, j, qk_bufs[0][0], qk_bufs[0][1])
    prog["conv"] = 0

    def conv_stream():
        for m in range(1, 4):
            if m >= 2:
                fw.il.helper_wait(lambda m=m: prog["main"] >= 4 * (m - 1) - 1)
            for j in range(8):
                conv_j(m, j, qk_bufs[m % 2][0], qk_bufs[m % 2][1])
            prog["conv"] = m

    pools["front"] = [0, 1]; pools["smp"] = [2]
    fw.il = Interleaver(ratio=1)
    fw.il.start(conv_stream)
    for c in range(16):
        mt = c // 4
        if c % 4 == 0:
            fw.il.main_wait(lambda mt=mt: prog["conv"] >= mt)
        cur = qk_bufs[mt % 2]
        Fq[c] = front(c, cur[0], cur[1])
        if c >= 1:
            Hq[c - 1] = back_a(c - 1, Fq.pop(c - 1))
            prog["main"] = c - 1
        if c >= 2:
            hm_to_T(Hq.pop(c - 2), 128, c - 2, 0)
    Hq[15] = back_a(15, Fq.pop(15))
    hm_to_T(Hq.pop(14), 128, 14, 0)
    hm_to_T(Hq.pop(15), 128, 15, 0)
    fw.il.drain()
    fw.il = None
    pools["front"] = [0, 1, 2]
    stop("p1c")
    ctr = fw.sb("ctr", [128, 4, 128])
    p_ = PF()
    for h in range(4):
        tr(p_, p_[:, h * 128:(h + 1) * 128], CT, CT[:, h, 0:128], 128, h == 3, f32=True)
    fw.op("act", lambda e: e.activation(ctr[:, :, :], p_[:, :512].rearrange("p (h k) -> p h k", h=4), AF.Copy), [p_], [ctr])
    fw.dma("sp", C_po, C_po[:, :, :].rearrange("h v k -> v h k"), ctr, ctr[:, :, :])
    fw.dma("sp", n_po, n_po[:, :].rearrange("h k -> k h"), CT, CT[:, :, 128], allow_slow_non_contiguous=True)

    stop("p1d")
    fw.pop_scope()
    fw.push_scope()
    sel16 = cload("sel16a", c_sel16, [16, 2048])
    hmt = Rot(fw, "hmts", [128, 512], BF16, 1); ej = Rot(fw, "ejs", [128, 128], BF16, 2)
    xs_v = [xnT_v[16]]
    ssl = slice(T, T + NS)
    us = fw.sb("us", [16, 2056])
    for blk in range(5):
        c0 = blk * 512; n = 512 if blk < 4 else 8
        wsrc = winA if blk < 4 else wgt
        p_ = PF()
        for kc in range(8):
            fw.op("pe", lambda e, p_=p_, kc=kc, c0=c0, n=n: e.matmul(p_[0:16, :n], xnT[:, kc, ssl], (winA[:, kc, c0:c0 + n] if blk < 4 else wgt[:, kc, 0:8]), start=(kc == 0), stop=(kc == 7)), [wsrc] + xs_v, [p_], inc=(kc == 7))
        fw.op("act", lambda e, p_=p_, c0=c0, n=n: e.activation(us[:, c0:c0 + n], p_[0:16, :n], AF.Copy), [p_], [us])
    ctmp = fw.sb("ctmp", [16, D]); qk_s = fw.sb("qk_s", [16, D])
    fw.push_scope()
    cs_in = fw.sb("cs_in", [16, 3, 512]); wcbc = fw.sb("wcbc", [16, 4, 512]); bcbc = fw.sb("bcbc", [16, 512])
    ca = fw.sb("ca", [16, 512])
    fw.dma("sp", conv_so, conv_so[:, 2, :], us, us[:, 0:D])
    for pc_ in range(2):
        cs_ = slice(pc_ * 512, (pc_ + 1) * 512)
        fw.dma("sp", cs_in, cs_in[:, :, :], conv_s, conv_s[:, :, cs_])
        fw.dma("sp", wcbc, wcbc[:, :, :], wcbc_d, wcbc_d[:, :, cs_])
        fw.dma("sp", bcbc, bcbc[:, :], bcbc_d, bcbc_d[:, cs_])
        fw.dma("sp", conv_so, conv_so[:, 0:2, cs_], cs_in, cs_in[:, 1:3, :], join=True)
        fw.op("dve", lambda e: e.tensor_tensor(ca[:, :], us[:, cs_], wcbc[:, 3, :], ALU.mult), [us, wcbc], [ca])
        fw.op("dve", lambda e: e.tensor_tensor(ca[:, :], ca[:, :], bcbc[:, :], ALU.add), [ca, bcbc], [ca])
        for k in range(3):
            fw.op("dve", lambda e, k=k: e.tensor_tensor(ctmp[:, 0:512], cs_in[:, k, :], wcbc[:, k, :], ALU.mult), [cs_in, wcbc], [ctmp])
            fw.op("dve", lambda e: e.tensor_tensor(ca[:, :], ca[:, :], ctmp[:, 0:512], ALU.add), [ca, ctmp], [ca])
        fw.op("act", lambda e: e.activation(ctmp[:, 0:512], ca[:, :], AF.Exp, scale=-1.0), [ca], [ctmp])
        fw.op("dve", lambda e: e.tensor_scalar_add(ctmp[:, 0:512], ctmp[:, 0:512], 1.0), [ctmp], [ctmp])
        fw.op("dve", lambda e: e.reciprocal(ctmp[:, 0:512], ctmp[:, 0:512]), [ctmp], [ctmp])
        fw.op("dve", lambda e: e.tensor_tensor(qk_s[:, cs_], ctmp[:, 0:512], ca[:, :], ALU.mult), [ctmp, ca], [qk_s])
    fw.pop_scope()
    fw.op("dve", lambda e: e.tensor_scalar_mul(qk_s[:, 512:D], qk_s[:, 512:D], KS), [qk_s], [qk_s])
    gsig_s = fw.sb("gsig_s", [16, 512])
    fw.op("act", lambda e: e.activation(gsig_s[:, :], us[:, 1536:2048], AF.Exp, scale=-1.0), [us], [gsig_s])
    fw.op("dve", lambda e: e.tensor_scalar_add(gsig_s[:, :], gsig_s[:, :], 1.0), [gsig_s], [gsig_s])
    fw.op("dve", lambda e: e.reciprocal(gsig_s[:, :], gsig_s[:, :]), [gsig_s], [gsig_s])
    fw.op("dve", lambda e: e.tensor_tensor(gsig_s[:, :], gsig_s[:, :], gmh[0:16, :], ALU.mult), [gsig_s, gmh], [gsig_s])
    sg_ = fw.sb("sg_", [16, 64]); scal = fw.sb("scal", [16, 12]); n_old = fw.sb("n_old", [16, 512]); n_new = fw.sb("n_new", [16, 512])
    fw.dma("sp", sg_, sg_[:, 8:12], m_s, m_s[:, :])
    fw.dma("sp", n_old, n_old[:, :], n_s, n_s[:, :])
    fw.op("dve", lambda e: e.tensor_tensor(sg_[:, 0:8], us[:, 2048:2056], bgbc[:, :], ALU.add), [us, bgbc], [sg_])
    fw.op("act", lambda e: e.activation(sg_[:, 4:8], sg_[:, 4:8], AF.Exp, scale=-1.0), [sg_], [sg_])
    fw.op("act", lambda e: e.activation(sg_[:, 4:8], sg_[:, 4:8], AF.Ln, bias=1.0), [sg_], [sg_])
    fw.op("dve", lambda e: e.tensor_tensor(sg_[:, 12:16], sg_[:, 8:12], sg_[:, 4:8], ALU.subtract), [sg_], [sg_])
    fw.op("dve", lambda e: e.tensor_tensor(sg_[:, 16:20], sg_[:, 12:16], sg_[:, 0:4], ALU.max), [sg_], [sg_])
    fw.dma("sp", m_so, m_so[:, :], sg_, sg_[:, 16:20])
    fw.op("dve", lambda e: e.tensor_tensor(sg_[:, 20:24], sg_[:, 12:16], sg_[:, 16:20], ALU.subtract), [sg_], [sg_])
    fw.op("act", lambda e: e.activation(scal[:, 0:4], sg_[:, 20:24], AF.Exp), [sg_], [scal])
    fw.op("dve", lambda e: e.tensor_tensor(sg_[:, 20:24], sg_[:, 0:4], sg_[:, 16:20], ALU.subtract), [sg_], [sg_])
    fw.op("act", lambda e: e.activation(scal[:, 4:8], sg_[:, 20:24], AF.Exp), [sg_], [scal])
    fw.op("act", lambda e: e.activation(sg_[:, 24:28], sg_[:, 16:20], AF.Exp, scale=-1.0), [sg_], [sg_])
    fw.op("dve", lambda e: e.tensor_tensor(ctmp[:, 0:512], qk_s[:, 0:512], qk_s[:, 512:D], ALU.mult), [qk_s], [ctmp])
    fw.op("dve", lambda e: e.tensor_reduce(sg_[:, 28:32], ctmp[:, 0:512].rearrange("p (h k) -> p h k", h=4), AX.X, ALU.add), [ctmp], [sg_])
    fw.op("dve", lambda e: e.tensor_tensor(ctmp[:, 512:D], qk_s[:, 0:512], n_old[:, :], ALU.mult), [qk_s, n_old], [ctmp])
    fw.op("dve", lambda e: e.tensor_reduce(sg_[:, 32:36], ctmp[:, 512:D].rearrange("p (h k) -> p h k", h=4), AX.X, ALU.add), [ctmp], [sg_])
    fw.op("dve", lambda e: e.tensor_tensor(scal[:, 8:12], scal[:, 4:8], sg_[:, 28:32], ALU.mult), [scal, sg_], [scal])
    fw.op("dve", lambda e: e.tensor_tensor(sg_[:, 36:40], scal[:, 0:4], sg_[:, 32:36], ALU.mult), [scal, sg_], [sg_])
    fw.op("dve", lambda e: e.tensor_tensor(sg_[:, 36:40], sg_[:, 36:40], scal[:, 8:12], ALU.add), [scal, sg_], [sg_])
    fw.op("dve", lambda e: e.tensor_tensor(n_new[:, :].rearrange("p (h k) -> p h k", h=4), n_old[:, :].rearrange("p (h k) -> p h k", h=4), scal[:, 0:4].unsqueeze(2).to_broadcast([16, 4, 128]), ALU.mult), [n_old, scal], [n_new])
    fw.op("dve", lambda e: e.tensor_tensor(ctmp[:, 0:512].rearrange("p (h k) -> p h k", h=4), qk_s[:, 512:D].rearrange("p (h k) -> p h k", h=4), scal[:, 4:8].unsqueeze(2).to_broadcast([16, 4, 128]), ALU.mult), [qk_s, scal], [ctmp])
    fw.op("dve", lambda e: e.tensor_tensor(n_new[:, :], n_new[:, :], ctmp[:, 0:512], ALU.add), [n_new, ctmp], [n_new])
    fw.dma("sp", n_so, n_so[:, :], n_new, n_new[:, :])
    vT_s = fw.sb("vT_s", [128, 4, 16]); p_ = PF()
    for h in range(4):
        tr(p_, p_[:, h * 16:(h + 1) * 16], us, us[0:16, 1024 + h * 128:1024 + (h + 1) * 128], 16, h == 3, f32=True)
    fw.op("act", lambda e: e.activation(vT_s[:, :, :], p_[:, 0:64].rearrange("p (h b) -> p h b", h=4), AF.Copy), [p_], [vT_s])
    sc_all = fw.sb("sc_all", [128, 16, 12]); Cq = fw.sb("Cq", [128, 16, 4])
    Cb = Rot(fw, "Cb", [128, 4, 128], F32, 2); Cn = Rot(fw, "Cn", [128, 4, 128], F32, 2); ct1 = Rot(fw, "ct1", [128, 512], F32, 2)
    wv = Rot(fw, "wv", [128, 4], F32, 2)
    for b in range(NS):
        psq = PF(); psk = PF(); pss = PF()
        mm(psq, psq[:, :512], sel16, sel16[:, b * 128:(b + 1) * 128], qk_s, qk_s[:, 0:512], True, True, True)
        mm(psk, psk[:, :512], sel16, sel16[:, b * 128:(b + 1) * 128], qk_s, qk_s[:, 512:D], True, True, True)
        mm(pss, pss[:, :12], sel16, sel16[:, b * 128:(b + 1) * 128], scal, scal[:, :], True, True, True)
        fw.op("act", lambda e, b=b, pss=pss: e.activation(sc_all[:, b, :], pss[:, 0:12], AF.Copy), [pss], [sc_all])
        cb_ = Cb(); cn_ = Cn(); t_ = ct1(); w_ = wv()
        fw.dma("sp", cb_, cb_[:, :, :], C_s, C_s[b].rearrange("h v k -> v h k"))
        fw.op("dve", lambda e, cb_=cb_, t_=t_, psq=psq: e.tensor_tensor(t_[:, :], cb_[:, :, :].rearrange("p h k -> p (h k)"), psq[:, :512], ALU.mult), [cb_, psq], [t_])
        fw.op("dve", lambda e, t_=t_, b=b: e.tensor_reduce(Cq[:, b, :], t_[:, :].rearrange("p (h k) -> p h k", h=4), AX.X, ALU.add), [t_], [Cq])
        fw.op("dve", lambda e, w_=w_, b=b: e.tensor_tensor(w_[:, :], vT_s[:, :, b], sc_all[:, b, 4:8], ALU.mult), [vT_s, sc_all], [w_])
        fw.op("pool", lambda e, cn_=cn_, cb_=cb_, b=b: e.tensor_tensor(cn_[:, :, :], cb_[:, :, :], sc_all[:, b, 0:4].unsqueeze(2).to_broadcast([128, 4, 128]), ALU.mult), [cb_, sc_all], [cn_])
        fw.op("dve", lambda e, t_=t_, psk=psk, w_=w_: e.tensor_tensor(t_[:, :].rearrange("p (h k) -> p h k", h=4), psk[:, :512].rearrange("p (h k) -> p h k", h=4), w_[:, :].unsqueeze(2).to_broadcast([128, 4, 128]), ALU.mult), [psk, w_], [t_])
        fw.op("pool", lambda e, cn_=cn_, t_=t_: e.tensor_tensor(cn_[:, :, :], cn_[:, :, :], t_[:, :].rearrange("p (h k) -> p h k", h=4), ALU.add), [cn_, t_], [cn_])
        fw.dma("sp", C_so, C_so[b].rearrange("h v k -> v h k"), cn_, cn_[:, :, :], join=True)
    numT = fw.sb("numT", [128, 16, 4]); nt2 = fw.sb("nt2", [128, 16, 4])
    fw.op("dve", lambda e: e.tensor_tensor(numT[:, :, :], vT_s[:, :, :].rearrange("p h b -> p b h"), sc_all[:, :, 8:12], ALU.mult), [vT_s, sc_all], [numT])
    fw.op("dve", lambda e: e.tensor_tensor(nt2[:, :, :], Cq[:, :, :], sc_all[:, :, 0:4], ALU.mult), [Cq, sc_all], [nt2])
    fw.op("dve", lambda e: e.tensor_tensor(numT[:, :, :], numT[:, :, :], nt2[:, :, :], ALU.add), [numT, nt2], [numT])
    p_ = PF()
    for h in range(4):
        tr(p_, p_[0:16, h * 128:(h + 1) * 128], numT, numT[:, :, h], 128, h == 3, f32=True)
    hm = hmt()
    epilogue(16, [p_[0:16, h * 128:(h + 1) * 128] for h in range(4)], [p_], sg_[:, 36:40], sg_, sg_[:, 24:28], sg_, gsig_s, hm)
    hm_to_T(hm, 16, 16, 0)
    fw.pop_scope()
    fw.pop_scope()

    stop("p1e")
    fw.push_scope()
    winB = fw.sb("winB", [128, 8, 2048], BF16)
    winB_v = [fw.view(winB, f"winB{i}") for i in range(4)]
    for blk in range(4):
        for k0 in range(0, 8, 4):
            fw.dma("pool", winB_v[blk], winB[:, k0:k0 + 4, blk * 512:(blk + 1) * 512], w_in,
                   w_in[k0 * 128:(k0 + 4) * 128, 2056 + blk * 512:2056 + (blk + 1) * 512].rearrange("(k p) c -> p k c", p=128), join=True)
    rvt = Rot(fw, "rvt", [128, 512], BF16, 2); gsil = Rot(fw, "gsil", [128, 512], F32, 2); sgr = Rot(fw, "sgr", [128, 512], F32, 2)
    xr = Rot(fw, "xr", [128, 512], F32, 2); rt = Rot(fw, "rt", [128, 256], F32, 4)
    rqt = Rot(fw, "rqt", [128, 512], BF16, 2); rkt = Rot(fw, "rkt", [128, 512], BF16, 2)
    rqT = Rot(fw, "rqT", [128, 512], BF16, 2); rqdT = Rot(fw, "rqdT", [128, 512], BF16, 2); rkT = Rot(fw, "rkT", [128, 512], BF16, 2)
    kdt = Rot(fw, "kdt", [128, 4, 128], BF16, 2); wts = Rot(fw, "wtsr", [128, 512], BF16, 2); hmt = Rot(fw, "hmtr", [128, 512], BF16, 3)
    Sst = fw.sb("Sst", [128, 512]); Sbf = fw.sb("Sbf", [128, 512], BF16)
    ej = Rot(fw, "ejr", [128, 128], BF16, 2)
    fw.op("pool", lambda e: e.memset(Sst[:], 0.0), [], [Sst])
    fw.op("pool", lambda e: e.memset(Sbf[:], 0.0), [], [Sbf])
    sel16 = cload("sel16b", c_sel16, [16, 2048])
    rq_s = fw.sb("rq_s", [16, 512]); rk_s = fw.sb("rk_s", [16, 512]); rv_s = fw.sb("rv_s", [16, 512]); gsil_s = fw.sb("gsil_s", [16, 512])

    def rope(R, tt, src, dst, dst_dt_bf):
        sv = src[:R, :].rearrange("p (h a j) -> p h a j", h=4, a=2)
        dv = dst[:R, :].rearrange("p (h a j) -> p h a j", h=4, a=2)
        cb_ = cosT[:R, tt, :].unsqueeze(1).to_broadcast([R, 4, 64]); sb_ = sinT[:R, tt, :].unsqueeze(1).to_broadcast([R, 4, 64])
        t1 = rt(); t2 = rt(); t3 = rt(); t4 = rt()
        v4 = lambda t: t[:R, :].rearrange("p (h j) -> p h j", h=4)
        fw.op("dve", lambda e: e.tensor_tensor(v4(t1), sv[:, :, 0, :], cb_, ALU.mult), [src, cosT], [t1])
        fw.op("pool", lambda e: e.tensor_tensor(v4(t2), sv[:, :, 1, :], sb_, ALU.mult), [src, sinT], [t2])
        fw.op("dve", lambda e: e.tensor_tensor(dv[:, :, 0, :], v4(t1), v4(t2), ALU.subtract), [t1, t2], [dst])
        fw.op("pool", lambda e: e.tensor_tensor(v4(t3), sv[:, :, 1, :], cb_, ALU.mult), [src, cosT], [t3])
        fw.op("dve", lambda e: e.tensor_tensor(v4(t4), sv[:, :, 0, :], sb_, ALU.mult), [src, sinT], [t4])
        fw.op("pool", lambda e: e.tensor_tensor(dv[:, :, 1, :], v4(t3), v4(t4), ALU.add), [t3, t4], [dst])

    def rfront(tt):
        R = rows(tt); tsl = slice(tt * 128, tt * 128 + R); xv = [xnT_v[tt]]
        F = {}
        for blk in range(4):
            p_ = PF("front")
            for kc in range(8):
                fw.op("pe", lambda e, p_=p_, kc=kc, blk=blk: e.matmul(p_[:R, :512], xnT[:, kc, tsl], winB[:, kc, blk * 512:(blk + 1) * 512], start=(kc == 0), stop=(kc == 7)), [winB_v[blk]] + xv, [p_], inc=(kc == 7))
            if blk == 0 or blk == 1:
                x_ = xr()
                fw.op("act", lambda e, p_=p_, x_=x_, blk=blk: e.activation(x_[:R, :], p_[:R, :512], AF.Copy, scale=(1.0 if blk == 0 else KS)), [p_], [x_])
                if tt < 16:
                    d_ = rqt() if blk == 0 else rkt()
                else:
                    d_ = rq_s if blk == 0 else rk_s
                rope(R, tt, x_, d_, tt < 16)
                F["rq_" if blk == 0 else "rk_"] = d_
            elif blk == 2:
                if tt < 16:
                    rv_ = rvt()
                    fw.op("act", lambda e, p_=p_, rv_=rv_: e.activation(rv_[:R, :], p_[:R, :512], AF.Copy), [p_], [rv_])
                    F["rv_"] = rv_
                else:
                    fw.op("act", lambda e, p_=p_: e.activation(rv_s[:R, :], p_[:R, :512], AF.Copy), [p_], [rv_s])
            else:
                g_ = sgr(); gs = gsil() if tt < 16 else gsil_s
                sig_gate(R, p_[:R, :512], p_, g_, gs, grh, True)
                F["gs"] = gs
        if tt == 16:
            return F
        rq_, rk_ = F["rq_"], F["rk_"]
        qT_ = rqT(); qdT_ = rqdT(); kT_ = rkT(); kd_ = kdt()
        for (src_, dstT) in ((rq_, qT_), (rk_, kT_)):
            p2 = PB()
            for h in range(4):
                tr(p2, p2[:, h * 128:(h + 1) * 128], src_, src_[:, h * 128:(h + 1) * 128], 128, h == 3)
            fw.op("act", lambda e, p2=p2, dstT=dstT: e.activation(dstT[:, :], p2[:, 0:512], AF.Copy), [p2], [dstT])
        fw.op("pool", lambda e: e.tensor_tensor(qdT_[:, :], qT_[:, :], qdec[:, :], ALU.mult), [qT_, qdec], [qdT_])
        fw.op("pool", lambda e: e.tensor_tensor(kd_[:, :, :], rk_[:, :].rearrange("p (h k) -> p h k", h=4), kdec[:, :].unsqueeze(2).to_broadcast([128, 4, 128]), ALU.mult), [rk_, kdec], [kd_])
        ps_s = PF("front")
        for h in range(4):
            mm(ps_s, ps_s[:, h * 128:(h + 1) * 128], kT_, kT_[:, h * 128:(h + 1) * 128], qT_, qT_[:, h * 128:(h + 1) * 128], True, True, h == 3)
        w_ = wts()
        fw.op("dve", lambda e: e.tensor_tensor(w_[:, :], ps_s[:, :512], decayT[:, :], ALU.mult), [ps_s, decayT], [w_])
        F.update(qdT_=qdT_, kd_=kd_, w_=w_)
        return F

    def rback(tt, F):
        rv_, gs, w_, qdT_, kd_ = F["rv_"], F["gs"], F["w_"], F["qdT_"], F["kd_"]
        ps_o = PF("back")
        for h in range(4):
            hs = slice(h * 128, (h + 1) * 128)
            mm(ps_o, ps_o[:, hs], w_, w_[:, hs], rv_, rv_[:, hs], True, False, False)
            mm(ps_o, ps_o[:, hs], qdT_, qdT_[:, hs], Sbf, Sbf[:, hs], False, True, h == 3)
        ps_c = PF("back")
        for h in range(4):
            hs = slice(h * 128, (h + 1) * 128)
            mm(ps_c, ps_c[:, hs], kd_, kd_[:, h, :], rv_, rv_[:, hs], True, True, h == 3)
        fw.op("pool", lambda e: e.tensor_tensor(Sst[:, :].rearrange("p (h v) -> p h v", h=4), Sst[:, :].rearrange("p (h v) -> p h v", h=4), cdec[:, :].unsqueeze(2).to_broadcast([128, 4, 128]), ALU.mult), [Sst, cdec], [Sst])
        fw.op("dve", lambda e: e.tensor_tensor(Sst[:, :], Sst[:, :], ps_c[:, :512], ALU.add), [Sst, ps_c], [Sst])
        hm = hmt()
        epilogue(128, [ps_o[:, h * 128:(h + 1) * 128] for h in range(4)], [ps_o], None, None, None, None, gs, hm)
        bg_step(1)
        return hm

    rqT_s = fw.sb("rqT_s", [128, 4, 16]); rkT_s = fw.sb("rkT_s", [128, 4, 16])
    Sb_ = Rot(fw, "Sb_", [128, 4, 128], F32, 2); Sn_ = Rot(fw, "Sn_", [128, 4, 128], F32, 2); st1 = Rot(fw, "st1", [128, 4, 128], F32, 2)
    pso = pl; oT_s = fw.sb("oT_s", [128, 16, 4])

    def sample_ret_loop():
        for (src_, dstT) in ((rq_s, rqT_s), (rk_s, rkT_s)):
            p_ = PF()
            for h in range(4):
                tr(p_, p_[:, h * 16:(h + 1) * 16], src_, src_[0:16, h * 128:(h + 1) * 128], 16, h == 3, f32=True)
            fw.op("act", lambda e, p_=p_, dstT=dstT: e.activation(dstT[:, :, :], p_[:, 0:64].rearrange("p (h b) -> p h b", h=4), AF.Copy), [p_], [dstT])
        for b in range(NS):
            psv = PF()
            mm(psv, psv[:, :512], sel16, sel16[:, b * 128:(b + 1) * 128], rv_s, rv_s[:, :], True, True, True)
            s_ = Sb_(); sn = Sn_(); t_ = st1()
            fw.dma("sp", s_, s_[:, :, :], S_s, S_s[b].rearrange("h k v -> k h v"))
            fw.op("pool", lambda e, s_=s_, t_=t_: e.tensor_tensor(t_[:, :, :], s_[:, :, :], gam[:, :].unsqueeze(2).to_broadcast([128, 4, 128]), ALU.mult), [s_, gam], [t_])
            fw.op("dve", lambda e, sn=sn, psv=psv, b=b: e.tensor_tensor(sn[:, :, :], psv[:, :512].rearrange("p (h v) -> p h v", h=4), rkT_s[:, :, b].unsqueeze(2).to_broadcast([128, 4, 128]), ALU.mult), [psv, rkT_s], [sn])
            fw.op("pool", lambda e, sn=sn, t_=t_: e.tensor_tensor(sn[:, :, :], sn[:, :, :], t_[:, :, :], ALU.add), [sn, t_], [sn])
            fw.dma("sp", S_so, S_so[b].rearrange("h k v -> k h v"), sn, sn[:, :, :], join=True)
            for h in range(4):
                mm(pso, pso[:, b * 4 + h:b * 4 + h + 1], sn, sn[:, h, :], rqT_s, rqT_s[:, h, b:b + 1], True, True, h == 3)


    pools["back"] = [3, 4]; pools["smp"] = [5]
    rfront(16)
    fw.il = Interleaver(ratio=1)
    fw.il.start(sample_ret_loop)
    Fq = {}; Hq = {}
    for tt in range(16):
        Fq[tt] = rfront(tt)
        if 1 <= tt:
            Hq[tt - 1] = rback(tt - 1, Fq.pop(tt - 1))
            fw.op("act", lambda e: e.activation(Sbf[:, :], Sst[:, :], AF.Copy), [Sst], [Sbf])
        if 2 <= tt:
            hm_to_T(Hq.pop(tt - 2), 128, tt - 2, 4)
    Hq[15] = rback(15, Fq.pop(15))
    hm_to_T(Hq.pop(14), 128, 14, 4)
    hm_to_T(Hq.pop(15), 128, 15, 4)
    fw.il.drain()
    fw.il = None
    pools["back"] = [3, 4, 5, 6]
    fw.dma("sp", S_po, S_po[:, :, :].rearrange("h k v -> k h v"), Sst, Sst[:, :].rearrange("p (h v) -> p h v", h=4))
    stop("p2a")
    fw.op("act", lambda e: e.activation(oT_s[:, :, :], pso[:, 0:64].rearrange("p (b h) -> p b h", h=4), AF.Copy), [pso], [oT_s])
    p_ = PF()
    for h in range(4):
        tr(p_, p_[0:16, h * 128:(h + 1) * 128], oT_s, oT_s[:, :, h], 128, h == 3, f32=True)
    hm = hmt()
    epilogue(16, [p_[0:16, h * 128:(h + 1) * 128] for h in range(4)], [p_], None, None, None, None, gsil_s, hm)
    hm_to_T(hm, 16, 16, 4)
    fw.pop_scope()
    fw.pop_scope()

    stop("p2b")
    bg_step(1000)
    fw.push_scope()
    wp = Rot(fw, "wp", [128, 4096], BF16, 3)
    xres = [fw.sb(f"xres{i}", [128, D]) for i in range(5)]
    xn2T = fw.sb("xn2T", [128, 8, 528], BF16); xn3T = fw.sb("xn3T", [128, 8, 528], BF16)
    qcT = fw.sb("qcT", [128, 8, 512], BF16); oT = fw.sb("oT", [128, 8, 528], BF16); oT_sv = fw.view(oT, "oT_sv"); hT = fw.sb("hT", [128, 22, 528], BF16)
    pT = Rot(fw, "pT", [128, 2, 512], BF16, 2); rd = Rot(fw, "rd", [128, 512], F32, 2); sgf = Rot(fw, "sgf", [128, 512], F32, 2)
    qc_s = fw.sb("qc_s", [16, D]); ysb = Rot(fw, "ysb", [128, D], F32, 2)
    sps = Rot(fw, "sps", [128, 16], F32, 2); aj = Rot(fw, "aj", [128, 256], F32, 2)
    rden_s = fw.sb("rden_s", [128, 16, 4])
    Kb = Rot(fw, "Kb", [128, 2048], F32, 1); Vb = Rot(fw, "Vb", [128, 2048], BF16, 2); spb = Rot(fw, "spb", [128, 8], BF16, 2)
    sel16 = cload("sel16c", c_sel16, [16, 2048])
    hin = Rot(fw, "hin", [128, 8, 128], BF16, 2)

    def wload(dram, ap):
        w_ = wp()
        k_, c_ = ap.shape[1], ap.shape[2]
        v_ = w_[:, 0:k_ * c_].rearrange("p (k c) -> p k c", k=k_)
        fw.dma("sp", w_, v_, dram, ap)
        return w_, v_

    pools["all"] = [0, 1, 2, 3]; pools["smp"] = [4, 5]

    def sample_attn():
        pden = pl; poT = pl
        for b in range(NS):
            kb = Kb(); vb = Vb(); s_ = sps(); sb16 = spb()
            kb3 = kb[:, :].rearrange("p (m c) -> p m c", m=2); vb3 = vb[:, :].rearrange("p (m c) -> p m c", m=2)
            fw.dma("pool", kb, kb3, ck_s, ck_s[b].rearrange("(m p) c -> p m c", p=128))
            fw.dma("pool", vb, vb3, cv_s, cv_s[b].rearrange("(m p) c -> p m c", p=128))
            pq = [PF(), PF()]
            for hf in range(2):
                mm(pq[hf], pq[hf][:, :512], sel16, sel16[:, b * 128:(b + 1) * 128], qc_s, qc_s[:, hf * 512:(hf + 1) * 512], True, True, True)
            fw.op("dve", lambda e, s_=s_: e.memset(s_[:, :], 0.0), [], [s_])
            for m2 in range(2):
                for h in range(4):
                    j_ = aj()
                    fw.op("dve", lambda e, j_=j_, kb=kb, m2=m2, h=h, s_=s_: e.scalar_tensor_tensor(j_[:, :], kb3[:, m2, h * 256:(h + 1) * 256], 1.0, pq[h // 2][:, (h % 2) * 256:(h % 2) * 256 + 256], ALU.mult, ALU.mult, accum_out=s_[:, m2 * 4 + h:m2 * 4 + h + 1]),
                          [kb, pq[h // 2]], [j_, s_])
            fw.op("act", lambda e, s_=s_, sb16=sb16: e.activation(sb16[:, 0:8], s_[:, 0:8], AF.Exp, scale=1.0 / 16.0), [s_], [sb16])
            for m2 in range(2):
                mm(pden, pden[:, b * 4:(b + 1) * 4], onesb, onesb[:, :], sb16, sb16[:, m2 * 4:4 + m2 * 4], m2 == 0, m2 == 1, m2 == 1)
            for ct in range(8):
                for m2 in range(2):
                    mm(poT, poT[:, 64 + ct * 16 + b:64 + ct * 16 + b + 1], vb, vb3[:, m2, ct * 128:(ct + 1) * 128], sb16, sb16[:, m2 * 4 + ct // 2:1 + m2 * 4 + ct // 2], m2 == 0, m2 == 1, (ct == 7 and m2 == 1))
        fw.op("dve", lambda e: e.reciprocal(rden_s[:, :, :], pden[:, 0:64].rearrange("p (b h) -> p b h", h=4)), [pden], [rden_s])
        for ct in range(8):
            fw.op("dve", lambda e, ct=ct: e.tensor_tensor(oT[:, ct, 512:528], poT[:, 64 + ct * 16:64 + (ct + 1) * 16], rden_s[:, :, ct // 2], ALU.mult), [poT, rden_s], [oT_sv])


    for ps_ in range(4):
        base_t = [4 * ps_ + i for i in range(4)]
        tiles_ab = base_t + ([16] if ps_ == 0 else [])
        tiles = base_t + ([16] if ps_ == 3 else [])
        ncol = 512 + (16 if ps_ == 3 else 0)
        off = lambda i: i * 128
        wo = [wload(wout_bf, wout_bf[:, :, hf * 512:(hf + 1) * 512]) for hf in range(2)]
        pend = None
        for i, tt in enumerate(tiles_ab):
            R = rows(tt); xt = xres[i]
            hi_ = hin()
            fw.dma("sp", hi_, hi_[:, :, :R], hmrT_d, hmrT_d[:, :, tt * 128:tt * 128 + R])
            if tt < 16:
                fw.dma("sp", xt, xt[:R, :], x_p, x_p[tt * 128:(tt + 1) * 128, :])
            else:
                fw.dma("sp", xt, xt[:R, :], x_s, x_s[:, :])
            for hf in range(2):
                p_ = PF()
                for kc in range(8):
                    mm(p_, p_[:R, :512], hi_, hi_[:, kc, :R], wo[hf][0], wo[hf][1][:, kc, :], kc == 0, kc == 7, kc == 7)
                fw.op("dve", lambda e, p_=p_, xt=xt, hf=hf, R=R: e.tensor_tensor(xt[:R, hf * 512:(hf + 1) * 512], xt[:R, hf * 512:(hf + 1) * 512], p_[:R, :512], ALU.add), [p_, xt], [xt])
            if pend is not None:
                pend()
            pend = norm_T(xt, R, G_XATTN, xn2T, xn2T[:, :, off(i):off(i) + R], defer=True)
        pend()
        for hf in range(2):
            wq_b, wq = wload(wcq_bf, wcq_bf[:, :, hf * 512:(hf + 1) * 512])
            for c4 in range(4):
                ct = hf * 4 + c4; p_ = PF()
                for kc in range(8):
                    mm(p_, p_[:, :512], wq_b, wq[:, kc, c4 * 128:(c4 + 1) * 128], xn2T, xn2T[:, kc, 0:512], kc == 0, kc == 7, kc == 7)
                fw.op("act", lambda e, p_=p_, ct=ct: e.activation(qcT[:, ct, :], p_[:, :512], AF.Copy), [p_], [qcT])
            if ps_ == 0:
                p_ = PF()
                for kc in range(8):
                    mm(p_, p_[0:16, :512], xn2T, xn2T[:, kc, 512:528], wq_b, wq[:, kc, :], kc == 0, kc == 7, kc == 7)
                fw.op("act", lambda e, p_=p_, hf=hf: e.activation(qc_s[:, hf * 512:(hf + 1) * 512], p_[0:16, :512], AF.Copy), [p_], [qc_s])
        def att_scores(h):
            pt = pT()
            for m2 in range(2):
                p_ = PF()
                for dc in range(2):
                    mm(p_, p_[:, :512], kmemT, kmemT[:, 2 * h + dc, m2 * 128:(m2 + 1) * 128], qcT, qcT[:, 2 * h + dc, :], dc == 0, dc == 1, dc == 1)
                fw.op("act", lambda e, p_=p_, pt=pt, m2=m2: e.activation(pt[:, m2, :], p_[:, :512], AF.Exp, scale=1.0 / 16.0), [p_], [pt])
            return pt

        def att_rest(h, pt):
            pd = PF(); r_ = rd()
            for m2 in range(2):
                mm(pd, pd[:, :512], onesb, onesb[:, :], pt, pt[:, m2, :], m2 == 0, m2 == 1, m2 == 1)
            fw.op("act", lambda e: e.activation(r_[:, :], pd[:, :512], AF.Ln), [pd], [r_])
            fw.op("act", lambda e: e.activation(r_[:, :], r_[:, :], AF.Exp, scale=-1.0), [r_], [r_])
            for dc in range(2):
                p_ = PF()
                for m2 in range(2):
                    mm(p_, p_[:, :512], vmem, vmem[:, m2, (2 * h + dc) * 128:(2 * h + dc + 1) * 128], pt, pt[:, m2, :], m2 == 0, m2 == 1, m2 == 1)
                fw.op("dve", lambda e, p_=p_, dc=dc: e.tensor_tensor(oT[:, 2 * h + dc, 0:512], p_[:, :512], r_[:, :], ALU.mult), [p_, r_], [oT])

        pts = {0: att_scores(0)}
        for h in range(4):
            if h + 1 < 4:
                pts[h + 1] = att_scores(h + 1)
            att_rest(h, pts.pop(h))
        if ps_ == 0:
            fw.il = Interleaver(ratio=1)
            fw.il.start(sample_attn)
        if ps_ == 3:
            fw.il.drain()
            fw.il = None
        wo = [wload(wco_bf, wco_bf[:, :, hf * 512:(hf + 1) * 512]) for hf in range(2)]
        pend = None
        for i, tt in enumerate(tiles):
            R = rows(tt); xt = xres[i]
            for hf in range(2):
                p_ = PF()
                for kc in range(8):
                    mm(p_, p_[:R, :512], (oT_sv if tt == 16 else oT), oT[:, kc, off(i):off(i) + R], wo[hf][0], wo[hf][1][:, kc, :], kc == 0, kc == 7, kc == 7)
                fw.op("dve", lambda e, p_=p_, xt=xt, hf=hf, R=R: e.tensor_tensor(xt[:R, hf * 512:(hf + 1) * 512], xt[:R, hf * 512:(hf + 1) * 512], p_[:R, :512], ALU.add), [p_, xt], [xt])
            if pend is not None:
                pend()
            pend = norm_T(xt, R, G_FFN, xn3T, xn3T[:, :, off(i):off(i) + R], defer=True)
        pend()
        for blk in range(6):
            c0 = blk * 512; n = min(512, DFF - c0)
            wg_b, wg = wload(wg_bf, wg_bf[:, :, c0:c0 + n]); wu_b, wu = wload(wu_bf, wu_bf[:, :, c0:c0 + n])
            for fi in range(n // 128):
                f = blk * 4 + fi
                pg = PF(); pu = PF(); g_ = sgf()
                for kc in range(8):
                    mm(pg, pg[:, :512], wg_b, wg[:, kc, fi * 128:(fi + 1) * 128], xn3T, xn3T[:, kc, 0:512], kc == 0, kc == 7, kc == 7)
                for kc in range(8):
                    mm(pu, pu[:, :512], wu_b, wu[:, kc, fi * 128:(fi + 1) * 128], xn3T, xn3T[:, kc, 0:512], kc == 0, kc == 7, kc == 7)
                fw.op("act", lambda e, pg=pg, g_=g_: e.activation(g_[:, :], pg[:, :512], AF.Silu), [pg], [g_])
                fw.op("dve", lambda e, pu=pu, g_=g_, f=f: e.tensor_tensor(hT[:, f, 0:512], g_[:, :], pu[:, :512], ALU.mult), [pu, g_], [hT])
                if ps_ == 3:
                    px = PF(); g2 = sgf()
                    for kc in range(8):
                        mm(px, px[:, 0:16], wg_b, wg[:, kc, fi * 128:(fi + 1) * 128], xn3T, xn3T[:, kc, 512:528], kc == 0, kc == 7, False)
                    for kc in range(8):
                        mm(px, px[:, 16:32], wu_b, wu[:, kc, fi * 128:(fi + 1) * 128], xn3T, xn3T[:, kc, 512:528], kc == 0, kc == 7, kc == 7)
                    fw.op("act", lambda e, px=px, g2=g2: e.activation(g2[:, 0:16], px[:, 0:16], AF.Silu), [px], [g2])
                    fw.op("dve", lambda e, px=px, g2=g2, f=f: e.tensor_tensor(hT[:, f, 512:528], g2[:, 0:16], px[:, 16:32], ALU.mult), [px, g2], [hT])
        for g4 in range(0, 22, 4):
            nf = min(4, 22 - g4)
            wd_b, wd = wload(wd_bf, wd_bf[:, g4:g4 + nf, :])
            for i, tt in enumerate(tiles):
                R = rows(tt); xt = xres[i]
                for hf in range(2):
                    p_ = PF()
                    for fi in range(nf):
                        mm(p_, p_[:R, :512], hT, hT[:, g4 + fi, off(i):off(i) + R], wd_b, wd[:, fi, hf * 512:(hf + 1) * 512], fi == 0, fi == nf - 1, fi == nf - 1)
                    fw.op("dve", lambda e, p_=p_, xt=xt, hf=hf, R=R: e.tensor_tensor(xt[:R, hf * 512:(hf + 1) * 512], xt[:R, hf * 512:(hf + 1) * 512], p_[:R, :512], ALU.add), [p_, xt], [xt])
        for i, tt in enumerate(tiles):
            R = rows(tt); xt = xres[i]; sq = junk(); s_ = stat(); y_ = ysb()
            fw.op("dve", lambda e, s_=s_, R=R: e.memset(s_[:R, 0:1], 0.0), [], [s_])
            fw.op("act", lambda e, sq=sq, xt=xt, s_=s_, R=R: e.activation(sq[:R, :], xt[:R, :], AF.Square, accum_out=s_[:R, 0:1]), [xt], [sq, s_])
            rstd_of(s_[:R, 0:1], s_[:R, 1:2], s_, R, D)
            fw.op("dve", lambda e, y_=y_, xt=xt, s_=s_, R=R: e.scalar_tensor_tensor(y_[:R, :], xt[:R, :], s_[:R, 1:2], gfin[:R, :], ALU.mult, ALU.mult), [xt, s_, gfin], [y_])
            if tt < 16:
                fw.dma("pool", y_p, y_p[tt * 128:(tt + 1) * 128, :], y_, y_[:R, :], join=True)
            else:
                fw.dma("pool", y_s, y_s[:, :], y_, y_[:R, :])
    fw.pop_scope()
    for o_ in outs:
        fw.wait_all("sp", [o_])
    fw.barrier()


def _consts():
    f32 = np.float32
    c = {}
    c["c_ident"] = np.eye(128, dtype=f32)
    s_ = np.arange(128)[:, None]; t_ = np.arange(128)[None, :]
    c["c_maskb"] = np.where(s_ <= t_, 0.0, -30000.0).astype(f32)
    lg = np.log1p(-np.exp2(-5.0 - np.arange(4, dtype=np.float64)))
    dec = np.where(t_ >= s_, np.exp(lg[:, None, None] * np.maximum(t_ - s_, 0)), 0.0)
    c["c_decayT"] = np.ascontiguousarray(dec.transpose(1, 0, 2).reshape(128, 512)).astype(f32)
    qd = np.exp(lg[:, None] * (np.arange(128) + 1.0))
    c["c_qdec"] = np.ascontiguousarray(np.broadcast_to(qd.reshape(1, 512), (128, 512))).astype(f32)
    c["c_kdec"] = np.ascontiguousarray(np.exp(lg[None, :] * (127.0 - np.arange(128))[:, None])).astype(f32)
    c["c_cdec"] = np.ascontiguousarray(np.broadcast_to(np.exp(lg * 128.0)[None, :], (128, 4))).astype(f32)
    c["c_gam"] = np.ascontiguousarray(np.broadcast_to(np.exp(lg)[None, :], (128, 4))).astype(f32)
    sel4 = np.zeros((4, 4, 128), f32)
    for h in range(4):
        sel4[h, h, :] = 1.0
    c["c_sel4"] = sel4.reshape(4, 512)
    sel16 = np.zeros((16, 16, 128), f32)
    for b in range(16):
        sel16[b, b, :] = 1.0
    c["c_sel16"] = sel16.reshape(16, 2048)
    inv = (np.float32(10000.0) ** (-np.arange(64, dtype=f32) / np.float32(64))).astype(f32)
    pos = np.concatenate([np.arange(T, dtype=f32), np.full(128, 16384.0, f32)])
    ang = (pos[:, None] * inv[None, :]).astype(f32)
    c["c_cos"] = np.ascontiguousarray(np.cos(ang).astype(f32).reshape(17, 128, 64).transpose(1, 0, 2))
    c["c_sin"] = np.ascontiguousarray(np.sin(ang).astype(f32).reshape(17, 128, 64).transpose(1, 0, 2))
    return c


_NC_CACHE = {}


def kernel(x_prompt, x_sample, cache_mem_k, cache_mem_v, state_mlstm_conv, state_mlstm_C, state_mlstm_n, state_mlstm_m,
           state_ret_S, mem_prompt, w_in, b_gate, w_conv, b_conv, g_mix, g_mhead, g_rhead, w_out, g_xattn, g_mem,
           w_ck, w_cv, w_cq, w_co, g_ffn, w_gate, w_up, w_down, g_final):
    f32 = np.float32
    A = lambda a: np.ascontiguousarray(np.asarray(a, dtype=f32))
    if "nc" not in _NC_CACHE:
        _NC_CACHE["nc"] = build_nc()
    nc = _NC_CACHE["nc"]
    shared = _consts()
    col = lambda g: A(g).reshape(8, 128).T
    shared["gcols"] = A(np.stack([col(g_mix[0]), col(g_xattn[0]), col(g_ffn[0]), col(g_mem[0])], axis=1))
    shared["gmh_bc"] = A(np.broadcast_to(A(g_mhead[0])[None, :], (128, 512)))
    shared["grh_bc"] = A(np.broadcast_to(A(g_rhead[0])[None, :], (128, 512)))
    shared["gfin_bc"] = A(np.broadcast_to(A(g_final)[None, :], (128, D)))
    shared["bgate_col"] = A(A(b_gate[0]).reshape(2, 4).T)
    shared["bgate_bc"] = A(np.broadcast_to(A(b_gate[0])[None, :], (16, 8)))
    shared["wconv_col"] = A(A(w_conv[0]).reshape(4, 8, 128).transpose(2, 1, 0))
    shared["bconv_col"] = A(A(b_conv[0]).reshape(8, 128).T)
    shared["wconv_bc"] = A(np.broadcast_to(A(w_conv[0])[None], (16, 4, D)))
    shared["bconv_bc"] = A(np.broadcast_to(A(b_conv[0])[None], (16, D)))
    for k_, v_ in (("w_in", w_in), ("w_out", w_out), ("w_ck", w_ck), ("w_cv", w_cv), ("w_cq", w_cq), ("w_co", w_co),
                   ("w_gate", w_gate), ("w_up", w_up), ("w_down", w_down)):
        shared[k_] = A(v_[0])
    in_maps = []
    for i in range(8):
        sl = slice(16 * i, 16 * (i + 1))
        m = dict(shared)
        m["x_p"] = A(x_prompt[i]); m["x_s"] = A(x_sample[sl, 0]); m["mem_p"] = A(mem_prompt[i])
        m["ck_s"] = A(cache_mem_k[0, sl]).reshape(16, 256, D); m["cv_s"] = A(cache_mem_v[0, sl]).reshape(16, 256, D)
        m["conv_s"] = A(state_mlstm_conv[0, sl]); m["C_s"] = A(state_mlstm_C[0, sl]); m["n_s"] = A(state_mlstm_n[0, sl]).reshape(16, 512)
        m["m_s"] = A(state_mlstm_m[0, sl]); m["S_s"] = A(state_ret_S[0, sl])
        in_maps.append(m)
    res = run_bass_kernel_spmd(nc, in_maps, core_ids=list(range(8)))
    r = res.results
    g = lambda k: [np.asarray(r[i][k], dtype=f32) for i in range(8)]
    y_prompt = np.stack(g("y_p"))
    y_sample = np.concatenate(g("y_s")).reshape(128, 1, D)
    mk = np.stack(g("mk_o")).reshape(1, 8, 256, 4, 256)
    mv = np.stack(g("mv_o")).reshape(1, 8, 256, 4, 256)
    conv_p = np.stack(g("conv_po"))[None]
    C_p = np.stack(g("C_po"))[None]
    n_p = np.stack(g("n_po"))[None]
    m_p = np.stack(g("m_po")).reshape(1, 8, 4)
    S_p = np.stack(g("S_po"))[None]
    conv_s = np.concatenate(g("conv_so"))[None]
    C_s = np.concatenate(g("C_so"))[None]
    n_s = np.concatenate(g("n_so")).reshape(1, 128, 4, 128)
    m_s = np.concatenate(g("m_so"))[None]
    S_s = np.concatenate(g("S_so"))[None]
    return (y_prompt, y_sample, mk, mv, conv_p, C_p, n_p, m_p, S_p, conv_s, C_s, n_s, m_s, S_s)
```

```python
import contextlib
import threading
import numpy as np
import concourse.bass as bass
import concourse.mybir as mybir

F32 = mybir.dt.float32
BF16 = mybir.dt.bfloat16
AF = mybir.ActivationFunctionType
ALU = mybir.AluOpType
AX = mybir.AxisListType

SEG = 30000


class Buf:
    def __init__(self, fw, t, name):
        self.fw = fw
        self.t = t
        self.name = name
        self.w = []
        self.r = []
        self.sem = None
        self.is_dram = False
        self.is_psum = False
        self.dcount = 0

    def __getitem__(self, idx):
        return self.t[idx]


class Interleaver:
    def __init__(self, ratio=1):
        self.turn = threading.Semaphore(0)
        self.back = threading.Semaphore(0)
        self.alive = False
        self.thread = None
        self.err = None
        self.ratio = ratio
        self.credit = 0

    def start(self, fn):
        def run():
            self.turn.acquire()
            try:
                fn()
            except BaseException as e:
                self.err = e
            finally:
                self.alive = False
                self.back.release()
        self.alive = True
        self.thread = threading.Thread(target=run)
        self.thread.start()

    def in_helper(self):
        return self.thread is not None and threading.current_thread() is self.thread

    def after_op(self):
        if self.in_helper():
            self.credit -= 1
            if self.credit <= 0:
                self.back.release()
                self.turn.acquire()
                self.credit = self.ratio
        elif self.alive:
            self.turn.release()
            self.back.acquire()
            if self.err is not None:
                raise self.err

    def helper_wait(self, cond):
        while not cond():
            self.back.release()
            self.turn.acquire()
        self.credit = self.ratio

    def main_wait(self, cond):
        while self.alive and not cond():
            self.turn.release()
            self.back.acquire()
            if self.err is not None:
                raise self.err

    def drain(self):
        while self.alive:
            self.turn.release()
            self.back.acquire()
        if self.thread is not None:
            self.thread.join()
        if self.err is not None:
            raise self.err
        self.thread = None


class FW:
    def __init__(self, nc, stack):
        self.nc = nc
        self.stack = stack
        self.engs = {"pe": nc.tensor, "act": nc.scalar, "dve": nc.vector, "pool": nc.gpsimd,
                     "sp": nc.sync}
        self.count = {k: 0 for k in self.engs}
        self.esems = {k: [] for k in self.engs}
        self.known = {k: {} for k in self.engs}
        self.sems = {}
        self.nsem = 0
        self.bufs = []
        self.il = None

    def new_sem(self, name):
        s = self.stack.enter_context(self.nc.semaphore(name))
        self.nsem += 1
        return s

    def sb(self, name, shape, dt=F32):
        t = self.stack.enter_context(self.nc.sbuf_tensor(name, list(shape), dt))
        b = Buf(self, t, name)
        self.bufs.append(b)
        return b

    def ps(self, name, shape, dt=F32):
        t = self.stack.enter_context(self.nc.psum_tensor(name, list(shape), dt))
        b = Buf(self, t, name)
        b.is_psum = True
        self.bufs.append(b)
        return b

    def dram(self, name, shape, dt=F32, kind="Internal"):
        t = self.nc.dram_tensor(name, list(shape), dt, kind=kind)
        b = Buf(self, t, name)
        b.is_dram = True
        self.bufs.append(b)
        return b

    def view(self, buf, name=None):
        b = Buf(self, buf.t, name or buf.name + "_v")
        self.bufs.append(b)
        return b

    def _esem(self, eng, seq):
        si = (seq - 1) // SEG
        key = (eng, si)
        if key not in self.sems:
            self.sems[key] = self.new_sem(f"e_{eng}_{si}")
        return key, (seq - 1) % SEG + 1

    def _wait(self, eng, ev):
        key, val = ev
        kn = self.known[eng]
        if kn.get(key, 0) >= val:
            return
        if isinstance(key, tuple) and key[0] in self.engs:
            for (k2, v2) in list(kn.items()):
                if isinstance(k2, tuple) and k2[0] == key[0] and k2[1] > key[1]:
                    return
        self.engs[eng].wait_ge(self.sems[key], val)
        kn[key] = val

    def _deps(self, eng, reads, writes):
        evs = []
        for b in reads:
            evs += b.w
            if b.is_psum:
                evs += [e for e in b.r if not (isinstance(e[0], tuple) and e[0][0] == eng)]
        for b in writes:
            evs += b.w
            evs += b.r
        for ev in evs:
            if eng == "pe" and isinstance(ev[0], tuple) and ev[0][0] == "pe":
                continue
            self._wait(eng, ev)

    def _record(self, ev, reads, writes):
        for b in writes:
            b.w = [ev]
            b.r = []
        for b in reads:
            if b in writes:
                continue
            b.r = [e for e in b.r if e[0] != ev[0]] + [ev]

    def op(self, eng, fn, reads=(), writes=(), inc=True):
        reads = [b for b in reads if b is not None]
        writes = [b for b in writes if b is not None]
        self._deps(eng, reads, writes)
        seq = self.count[eng] + 1
        ev = self._esem(eng, seq)
        ins = fn(self.engs[eng])
        if inc:
            ins.then_inc(self.sems[ev[0]], 1)
            self.count[eng] = seq
        self._record(ev, reads, writes)
        if self.il is not None and not (eng == "pe" and not inc):
            self.il.after_op()
        return ins

    def dma(self, q, out_buf, out_ap, in_buf, in_ap, sem_buf=None, join=False, **kw):
        sb = sem_buf
        if sb is None:
            sb = in_buf if (out_buf.is_dram and not in_buf.is_dram) else out_buf
        kind = "sw" if q == "pool" else "hw"
        if sb.sem is None:
            sb.sem = {}
            sb.dcount = {}
        if kind not in sb.sem:
            sb.sem[kind] = self.new_sem("d_" + kind + "_" + sb.name)
            sb.dcount[kind] = 0
            self.sems[("dma", id(sb), kind)] = sb.sem[kind]
        key = ("dma", id(sb), kind)
        evs = list(in_buf.w) + list(out_buf.r)
        if join:
            evs += [e for e in out_buf.w if e[0] != key]
        else:
            evs += list(out_buf.w)
        for ev in evs:
            self._wait(q, ev)
        sb.dcount[kind] += 16
        ev = (key, sb.dcount[kind])
        self.engs[q].dma_start(out=out_ap, in_=in_ap, **kw).then_inc(sb.sem[kind], 16)
        if join:
            out_buf.w = [e for e in out_buf.w if e[0] != key] + [ev]
        else:
            out_buf.w = [ev]
            out_buf.r = []
        in_buf.r = [e for e in in_buf.r if e[0] != key] + [ev]
        if self.il is not None:
            self.il.after_op()
        return ev

    def wait_all(self, eng, bufs):
        for b in bufs:
            for ev in b.w + b.r:
                self._wait(eng, ev)

from concourse.bass_utils import run_bass_kernel_spmd

T = 2048
NS = 16
D = 1024
NTOK = T + NS
DFF = 2816
KS = 128 ** -0.5
EPS = 1e-6
G_MIX, G_XATTN, G_FFN, G_MEM = 0, 1, 2, 3


import os


class _Stop(Exception):
    pass


class Rot:
    def __init__(self, fw, name, shape, dt, n):
        self.b = [fw.sb(f"{name}{i}", shape, dt) for i in range(n)]
        self.i = 0

    def __call__(self):
        b = self.b[self.i % len(self.b)]
        self.i += 1
        return b


def _scope_patch(FWc):
    def push_scope(self):
        self.gstack = getattr(self, "gstack", self.stack)
        s = contextlib.ExitStack()
        s.__enter__()
        self._scopes = getattr(self, "_scopes", []) + [(s, self.stack)]
        self.stack = s

    def pop_scope(self):
        self.barrier()
        s, prev = self._scopes.pop()
        s.__exit__(None, None, None)
        self.stack = prev

    def new_sem(self, name):
        g = getattr(self, "gstack", self.stack)
        s = g.enter_context(self.nc.semaphore(name))
        self.nsem += 1
        return s

    def barrier(self):
        evs = []
        for eng in ("pe", "act", "dve", "pool"):
            if self.count[eng] > 0:
                evs.append(self._esem(eng, self.count[eng]))
        for b in self.bufs:
            if b.sem is not None:
                for kind, cntv in b.dcount.items():
                    if cntv > 0:
                        evs.append((("dma", id(b), kind), cntv))
        for eng in ("pe", "act", "dve", "pool", "sp"):
            for ev in evs:
                if eng == "pe" and ev[0][0] == "pe":
                    continue
                self._wait(eng, ev)
        self.bufs = [b for b in self.bufs]

    FWc.push_scope = push_scope
    FWc.pop_scope = pop_scope
    FWc.new_sem = new_sem
    FWc.barrier = barrier


_scope_patch(FW)


def build_nc():
    nc = bass.Bass("TRN2", target_bir_lowering=False)
    st = contextlib.ExitStack()
    with st:
        fw = FW(nc, st)
        fw.gstack = st
        nc.allow_low_precision("bf16 matmul operands with fp32 PSUM accumulation")
        try:
            _build(nc, fw)
        except _Stop:
            for o_ in fw.outs:
                fw.wait_all("sp", [o_])
            fw.barrier()
            while getattr(fw, "_scopes", []):
                s_, prev = fw._scopes.pop()
                s_.__exit__(None, None, None)
                fw.stack = prev
    return nc


def _build(nc, fw):
    I = lambda n, s: fw.dram(n, s, F32, kind="ExternalInput")
    O = lambda n, s: fw.dram(n, s, F32, kind="ExternalOutput")
    x_p = I("x_p", [T, D]); x_s = I("x_s", [NS, D]); mem_p = I("mem_p", [256, D])
    ck_s = I("ck_s", [NS, 256, D]); cv_s = I("cv_s", [NS, 256, D])
    conv_s = I("conv_s", [NS, 3, D]); C_s = I("C_s", [NS, 4, 128, 128]); n_s = I("n_s", [NS, 512])
    m_s = I("m_s", [NS, 4]); S_s = I("S_s", [NS, 4, 128, 128])
    w_in = I("w_in", [D, 4104]); w_out = I("w_out", [D, D]); w_ck = I("w_ck", [D, D]); w_cv = I("w_cv", [D, D])
    w_cq = I("w_cq", [D, D]); w_co = I("w_co", [D, D]); w_gate = I("w_gate", [D, DFF]); w_up = I("w_up", [D, DFF])
    w_down = I("w_down", [DFF, D])
    c_ident = I("c_ident", [128, 128]); c_maskb = I("c_maskb", [128, 128]); c_decayT = I("c_decayT", [128, 512])
    c_qdec = I("c_qdec", [128, 512]); c_kdec = I("c_kdec", [128, 4]); c_cdec = I("c_cdec", [128, 4]); c_gam = I("c_gam", [128, 4])
    c_sel4 = I("c_sel4", [4, 512]); c_sel16 = I("c_sel16", [16, 2048])
    c_cos = I("c_cos", [128, 17, 64]); c_sin = I("c_sin", [128, 17, 64])
    gcols_d = I("gcols", [128, 4, 8]); gmh_d = I("gmh_bc", [128, 512]); grh_d = I("grh_bc", [128, 512]); gfin_d = I("gfin_bc", [128, D])
    bgcol_d = I("bgate_col", [4, 2]); bgbc_d = I("bgate_bc", [16, 8]); wccol_d = I("wconv_col", [128, 8, 4]); bccol_d = I("bconv_col", [128, 8])
    wcbc_d = I("wconv_bc", [16, 4, D]); bcbc_d = I("bconv_bc", [16, D])

    y_p = O("y_p", [T, D]); y_s = O("y_s", [NS, D]); mk_o = O("mk_o", [256, D]); mv_o = O("mv_o", [256, D])
    conv_po = O("conv_po", [3, D]); C_po = O("C_po", [4, 128, 128]); n_po = O("n_po", [4, 128]); m_po = O("m_po", [4, 1])
    S_po = O("S_po", [4, 128, 128]); conv_so = O("conv_so", [NS, 3, D]); C_so = O("C_so", [NS, 4, 128, 128])
    n_so = O("n_so", [NS, 512]); m_so = O("m_so", [NS, 4]); S_so = O("S_so", [NS, 4, 128, 128])
    outs = [y_p, y_s, mk_o, mv_o, conv_po, C_po, n_po, m_po, S_po, conv_so, C_so, n_so, m_so, S_so]
    fw.outs = outs

    def stop(tag):
        if os.environ.get("MK_STOP") == tag:
            raise _Stop()

    wg_bf = fw.dram("wg_bf", [128, 8, DFF], BF16); wu_bf = fw.dram("wu_bf", [128, 8, DFF], BF16)
    wd_bf = fw.dram("wd_bf", [128, 22, D], BF16)
    wout_bf = fw.dram("wout_bf", [128, 8, D], BF16); wcq_bf = fw.dram("wcq_bf", [128, 8, D], BF16); wco_bf = fw.dram("wco_bf", [128, 8, D], BF16)

    def cload(name, d, shape, dt=F32, q="sp"):
        b = fw.sb(name, shape, F32)
        fw.dma(q, b, b[:], d, d[:])
        return b
    identf = cload("identf", c_ident, [128, 128]); maskb = cload("maskb", c_maskb, [128, 128])
    decayT = cload("decayT", c_decayT, [128, 512]); qdec = cload("qdec", c_qdec, [128, 512])
    kdec = cload("kdec", c_kdec, [128, 4]); cdec = cload("cdec", c_cdec, [128, 4]); gam = cload("gam", c_gam, [128, 4])
    cosT = cload("cosT", c_cos, [128, 17, 64]); sinT = cload("sinT", c_sin, [128, 17, 64])
    gcols = cload("gcols_s", gcols_d, [128, 4, 8]); gmh = cload("gmh", gmh_d, [128, 512]); grh = cload("grh", grh_d, [128, 512])
    gfin = cload("gfin", gfin_d, [128, D]); bgcol = cload("bgcol", bgcol_d, [4, 2]); bgbc = cload("bgbc", bgbc_d, [16, 8])
    wccol = cload("wccol", wccol_d, [128, 8, 4]); bccol = cload("bccol", bccol_d, [128, 8])
    identb = fw.sb("identb", [128, 128], BF16)
    fw.op("dve", lambda e: e.tensor_copy(identb[:], identf[:]), [identf], [identb])
    onesb = fw.sb("onesb", [128, 128], BF16); onesf = fw.sb("onesf", [128, 128])
    fw.op("dve", lambda e: e.memset(onesb[:], 1.0), [], [onesb])
    fw.op("dve", lambda e: e.memset(onesf[:], 1.0), [], [onesf])
    kmemT = fw.sb("kmemT", [128, 8, 256], BF16); vmem = fw.sb("vmem", [128, 2, D], BF16)
    hmrT_d = fw.dram("hmrT_d", [128, 8, NTOK], BF16)
    hstage = Rot(fw, "hstage", [128, 4, 128], BF16, 2)

    pf = [fw.ps(f"pf{i}", [128, 512]) for i in range(7)]
    pb = [fw.ps(f"pb{i}", [128, 1024], BF16) for i in range(1)]
    cnt = {"pf": 0, "pb": 0, "e": 0}

    pools = {"all": [0, 1, 2, 3, 4, 5], "front": [0, 1, 2], "back": [3, 4, 5, 6], "smp": [5]}

    def PF(pool="all"):
        if fw.il is not None and fw.il.in_helper():
            pool = "smp"
        k_ = "pf_" + pool
        cnt[k_] = cnt.get(k_, 0) + 1
        lst = pools[pool]
        return pf[lst[cnt[k_] % len(lst)]]

    def PB():
        cnt["pb"] += 1
        return pb[0]

    pl = pf[6]
    xtile = Rot(fw, "xt", [128, D], F32, 2)
    xnb = Rot(fw, "xnb", [128, D], BF16, 3)
    junk = Rot(fw, "junk", [128, D], BF16, 1)
    stat = Rot(fw, "stat", [128, 16], F32, 6)
    mhalf = fw.sb("mhalf", [128, 16])
    fw.op("pool", lambda e: e.memset(mhalf[:], -0.5), [], [mhalf])

    def mm(ob, oap, lb, lap, rb, rap, start, stop, fin):
        fw.op("pe", lambda e: e.matmul(oap, lap, rap, start=start, stop=stop), [lb, rb], [ob], inc=fin)

    def tr(ob, oap, ib, iap, R, fin, f32=False):
        idn = (identf if f32 else identb)
        fw.op("pe", lambda e: e.transpose(oap, iap, idn[:R, :R]), [ib, idn], [ob], inc=fin)

    def rows(tt):
        return 128 if tt < 16 else NS

    def rstd_of(ssq_ap, out_ap, sb_, R, n):
        w_ = ssq_ap.shape[1]
        fw.op("pool", lambda e: e.tensor_scalar(out_ap, ssq_ap, 1.0 / n, EPS, ALU.mult, ALU.add), [sb_], [sb_])
        fw.op("pool", lambda e: e.tensor_tensor(out_ap, out_ap, mhalf[:R, 0:w_], ALU.pow), [sb_, mhalf], [sb_])

    def sig_gate(R, src_ap, src_buf, e_buf, out_buf, mul_buf, with_x):
        fw.op("act", lambda e: e.activation(e_buf[:R, :], src_ap, AF.Exp, scale=-1.0), [src_buf], [e_buf])
        fw.op("act", lambda e: e.activation(e_buf[:R, :], e_buf[:R, :], AF.Ln, bias=1.0), [e_buf], [e_buf])
        fw.op("act", lambda e: e.activation(e_buf[:R, :], e_buf[:R, :], AF.Exp, scale=-1.0), [e_buf], [e_buf])
        if with_x:
            fw.op("dve", lambda e: e.tensor_tensor(e_buf[:R, :], e_buf[:R, :], src_ap, ALU.mult), [e_buf, src_buf], [e_buf])
        fw.op("pool", lambda e: e.tensor_tensor(out_buf[:R, :], e_buf[:R, :], mul_buf[:R, :], ALU.mult), [e_buf, mul_buf], [out_buf])

    def norm_T(src, R, gi, dst_buf, dst_ap, defer=False):
        sq = junk(); s_ = stat()
        fw.op("dve", lambda e: e.memset(s_[:R, 0:1], 0.0), [], [s_])
        fw.op("act", lambda e: e.activation(sq[:R, :], src[:R, :], AF.Square, accum_out=s_[:R, 0:1]), [src], [sq, s_])
        rstd_of(s_[:R, 0:1], s_[:R, 1:2], s_, R, D)
        xn = xnb()

        def part1b():
            fw.op("dve", lambda e: e.tensor_scalar_mul(xn[:R, :], src[:R, :], s_[:R, 1:2]), [src, s_], [xn])

        if defer != 2:
            part1b()

        def part2():
            p_ = PB()
            for kc in range(8):
                tr(p_, p_[:, kc * 128:kc * 128 + R], xn, xn[:R, kc * 128:(kc + 1) * 128], R, kc == 7)
            fw.op("dve", lambda e: e.tensor_tensor(dst_ap, p_[:].rearrange("p (k t) -> p k t", k=8)[:, :, :R],
                                                   gcols[:, gi, :].unsqueeze(2).to_broadcast([128, 8, R]), ALU.mult),
                  [p_, gcols], [dst_buf])
        if defer == 2:
            return part1b, part2
        if defer:
            return part2
        part2()
        return None

    cast_engs = ["pool", "dve", "act"]

    def cast(eng, ob, oap, ib, iap):
        if eng == "act":
            fw.op("act", lambda e: e.activation(oap, iap, AF.Copy), [ib], [ob])
        else:
            fw.op(eng, lambda e: e.tensor_copy(oap, iap), [ib], [ob])

    def wcast(dst, w, c0, c1, nsplit=4):
        n = c1 - c0
        step = max(1, 8 // nsplit)
        for k0 in range(0, 8, step):
            fw.dma("pool", dst, dst[:, k0:k0 + step, 0:n], w, w[k0 * 128:(k0 + step) * 128, c0:c1].rearrange("(k p) c -> p k c", p=128), join=True)

    bg_jobs = []
    for (w, dst) in ((w_out, wout_bf), (w_cq, wcq_bf), (w_co, wco_bf), (w_gate, wg_bf), (w_up, wu_bf)):
        for k0 in range(0, 8, 2):
            bg_jobs.append((dst, dst[:, k0:k0 + 2, :], w, w[k0 * 128:(k0 + 2) * 128, :].rearrange("(k p) c -> p k c", p=128)))
    for f0 in range(0, 22, 2):
        bg_jobs.append((wd_bf, wd_bf[:, f0:f0 + 2, :], w_down, w_down[f0 * 128:(f0 + 2) * 128, :].rearrange("(f p) c -> p f c", p=128)))

    def bg_step(n=1):
        for _ in range(n):
            if bg_jobs:
                d_, dap, w_, wap = bg_jobs.pop(0)
                fw.dma("pool", d_, dap, w_, wap, join=True)

    stop("c0")
    fw.push_scope()
    mnT = fw.sb("mnT", [128, 8, 256], BF16)
    wck = fw.sb("wck", [128, 8, D], BF16); wcv = fw.sb("wcv", [128, 8, D], BF16)
    for t2 in range(2):
        xt = xtile()
        fw.dma("sp", xt, xt[:, :], mem_p, mem_p[t2 * 128:(t2 + 1) * 128, :])
        norm_T(xt, 128, G_MEM, mnT, mnT[:, :, t2 * 128:(t2 + 1) * 128])
    stop("p0a")
    wcast(wck, w_ck, 0, D)
    wcast(wcv, w_cv, 0, D)
    stop("p0b")
    for (w, outd, isv) in ((wck, mk_o, False), (wcv, mv_o, True)):
        for t2 in range(2):
            ot = xtile()
            for half in range(2):
                p_ = PF()
                for kc in range(8):
                    mm(p_, p_[:, :512], mnT, mnT[:, kc, t2 * 128:(t2 + 1) * 128], w, w[:, kc, half * 512:(half + 1) * 512], kc == 0, kc == 7, kc == 7)
                fw.op("act", lambda e, p_=p_, half=half, ot=ot: e.activation(ot[:, half * 512:(half + 1) * 512], p_[:, :512], AF.Copy), [p_], [ot])
                if isv:
                    fw.op("dve", lambda e, p_=p_, half=half, t2=t2: e.tensor_copy(vmem[:, t2, half * 512:(half + 1) * 512], p_[:, :512]), [p_], [vmem])
            fw.dma("sp", outd, outd[t2 * 128:(t2 + 1) * 128, :], ot, ot[:, :], join=True)
    stop("p0c")
    for ct in range(8):
        p_ = PF()
        for kc in range(8):
            mm(p_, p_[:, :256], wck, wck[:, kc, ct * 128:(ct + 1) * 128], mnT, mnT[:, kc, :], kc == 0, kc == 7, kc == 7)
        fw.op("act", lambda e, p_=p_, ct=ct: e.activation(kmemT[:, ct, :], p_[:, :256], AF.Copy), [p_], [kmemT])
    fw.pop_scope()

    stop("p0")
    fw.push_scope()
    xnT = fw.sb("xnT", [128, 8, NTOK], BF16)
    xnT_v = [fw.view(xnT, f"xnT{i}") for i in range(17)]
    pend1 = None
    for tt in range(17):
        R = rows(tt)
        xt = xtile()
        if tt < 16:
            fw.dma("sp", xt, xt[:R, :], x_p, x_p[tt * 128:(tt + 1) * 128, :])
        else:
            fw.dma("sp", xt, xt[:R, :], x_s, x_s[:, :])
        r_ = norm_T(xt, R, G_MIX, xnT_v[tt], xnT[:, :, tt * 128:tt * 128 + R], defer=2)
        if pend1 is not None:
            pend1()
        r_[0]()
        pend1 = r_[1]
    pend1()
    XV = lambda mt: xnT_v[mt * 4:(mt + 1) * 4]
    stop("p1a")

    fw.push_scope()
    wgt = fw.sb("wgt", [128, 8, 8], BF16)
    wcast(wgt, w_in, 2048, 2056, nsplit=1)

    GT = fw.sb("GT", [128, 16, 68]); GT2 = fw.sb("GT2", [128, 16, 4])
    winA = fw.sb("winA", [128, 8, 2048], BF16)
    wcast(winA, w_in, 0, 2048)
    fw.push_scope()
    gA = fw.sb("gA", [4, T]); gF = fw.sb("gF", [4, T]); Bn = fw.sb("Bn", [4, T]); Mh = fw.sb("Mh", [4, T])
    ones4 = fw.sb("ones4", [4, T]); TP = fw.sb("TP", [68, T]); msm = fw.sb("msm", [4, 64]); TP2 = fw.sb("TP2", [4, T])
    fw.op("pool", lambda e: e.memset(ones4[:], 1.0), [], [ones4])
    fw.op("pool", lambda e: e.memset(TP[:], 0.0), [], [TP])
    for mt in range(4):
        for gi_, dstb in ((0, gA), (1, gF)):
            p_ = PF()
            for kc in range(8):
                fw.op("pe", lambda e, p_=p_, kc=kc, gi_=gi_, mt=mt: e.matmul(p_[0:4, :512], wgt[:, kc, 4 * gi_:4 + 4 * gi_], xnT[:, kc, mt * 512:(mt + 1) * 512], start=(kc == 0), stop=(kc == 7)),
                      [wgt] + XV(mt), [p_], inc=(kc == 7))
            fw.op("act", lambda e, p_=p_, dstb=dstb, gi_=gi_, mt=mt: e.activation(dstb[:, mt * 512:(mt + 1) * 512], p_[0:4, :512], AF.Identity, bias=bgcol[:, gi_:gi_ + 1]), [p_, bgcol], [dstb])
    fw.op("act", lambda e: e.activation(gF[:], gF[:], AF.Exp, scale=-1.0), [gF], [gF])
    fw.op("act", lambda e: e.activation(gF[:], gF[:], AF.Ln, bias=1.0), [gF], [gF])
    fw.op("dve", lambda e: e.tensor_tensor_scan(Bn[:], ones4[:], gF[:], 0.0, ALU.mult, ALU.add), [ones4, gF], [Bn])
    fw.op("dve", lambda e: e.tensor_tensor(gA[:], gA[:], Bn[:], ALU.add), [gA, Bn], [gA])
    fw.op("dve", lambda e: e.tensor_tensor_scan(Mh[:], ones4[:], gA[:], 0.0, ALU.mult, ALU.max), [ones4, gA], [Mh])
    Mh3 = Mh[:].rearrange("p (c l) -> p c l", l=128)
    fw.op("dve", lambda e: e.memset(msm[:], 0.0), [], [msm])
    fw.op("dve", lambda e: e.tensor_copy(msm[:, 1:16], Mh3[:, 0:15, 127]), [Mh], [msm])
    fw.op("dve", lambda e: e.tensor_copy(msm[:, 16:32], Mh3[:, :, 127]), [Mh], [msm])
    fw.op("dve", lambda e: e.tensor_tensor(msm[:, 48:64], msm[:, 0:16], msm[:, 16:32], ALU.subtract), [msm], [msm])
    fw.op("act", lambda e: e.activation(msm[:, 48:64], msm[:, 48:64], AF.Exp), [msm], [msm])
    fw.op("dve", lambda e: e.tensor_copy(TP[0:4, :].rearrange("p (c l) -> p c l", l=128), msm[:, 48:64].unsqueeze(2).to_broadcast([4, 16, 128])), [msm], [TP])
    fw.op("dve", lambda e: e.tensor_tensor(gF[:].rearrange("p (c l) -> p c l", l=128), gA[:].rearrange("p (c l) -> p c l", l=128),
                                           msm[:, 16:32].unsqueeze(2).to_broadcast([4, 16, 128]), ALU.subtract), [gA, msm], [gF])
    fw.op("act", lambda e: e.activation(gF[:], gF[:], AF.Exp), [gF], [gF])
    fw.op("dve", lambda e: e.tensor_copy(TP[32:36, :], gF[:]), [gF], [TP])
    fw.op("dve", lambda e: e.tensor_tensor(gF[:].rearrange("p (c l) -> p c l", l=128), Mh3, msm[:, 16:32].unsqueeze(2).to_broadcast([4, 16, 128]), ALU.subtract), [Mh, msm], [gF])
    fw.op("act", lambda e: e.activation(TP2[:], gF[:], AF.Exp, scale=-1.0), [gF], [TP2])
    fw.op("dve", lambda e: e.tensor_tensor(gF[:], Bn[:], Mh[:], ALU.subtract), [Bn, Mh], [gF])
    fw.op("dve", lambda e: e.tensor_scalar_mul(msm[:, 32:33], gF[:, T - 1:T], -1.0), [gF], [msm])
    fw.dma("sp", m_po, m_po[:, :], msm, msm[:, 32:33])
    fw.op("act", lambda e: e.activation(gF[:], gF[:], AF.Exp), [gF], [gF])
    fw.op("dve", lambda e: e.tensor_copy(TP[64:68, :], gF[:]), [gF], [TP])
    for c in range(16):
        p_ = PF()
        tr(p_, p_[:, 64:68], TP2, TP2[0:4, c * 128:(c + 1) * 128], 4, True, f32=True)
        fw.op("act", lambda e, p_=p_, c=c: e.activation(GT2[:, c, :], p_[:, 64:68], AF.Copy), [p_], [GT2])
    for c in range(16):
        p_ = PF()
        tr(p_, p_[:, 0:68], TP, TP[0:68, c * 128:(c + 1) * 128], 68, True, f32=True)
        fw.op("act", lambda e, p_=p_, c=c: e.activation(GT[:, c, :], p_[:, 0:68], AF.Copy), [p_], [GT])
    fw.pop_scope()

    stop("p1b")
    fw.push_scope()
    pcj = Rot(fw, "pcj", [128, 515], F32, 2); halo = fw.sb("halo", [128, 8, 3]); acc = Rot(fw, "acc", [128, 512], F32, 2); sgk = Rot(fw, "sgk", [128, 512], F32, 1)
    fw.op("pool", lambda e: e.memset(halo[:, :, :], 0.0), [], [halo])
    qTr = Rot(fw, "qT", [128, 4, 512], BF16, 2); kTr = Rot(fw, "kT", [128, 4, 512], BF16, 2)
    vext = Rot(fw, "vext", [128, 4, 129], BF16, 2)
    for b_ in vext.b:
        fw.op("pool", lambda e, b_=b_: e.memset(b_[:, :, 128:129], 1.0), [], [b_])
    gsig = Rot(fw, "gsig", [128, 512], F32, 2); sgo = Rot(fw, "sgo", [128, 512], F32, 1)
    nm = Rot(fw, "nm", [128, 512], F32, 1); DTt = Rot(fw, "DTt", [128, 512], F32, 1)
    wts = Rot(fw, "wts", [128, 512], BF16, 2); wbc = Rot(fw, "wbc", [128, 512], F32, 2)
    qw = Rot(fw, "qw", [128, 4, 128], BF16, 2); vw = Rot(fw, "vw", [128, 4, 129], BF16, 2)
    ktok = Rot(fw, "ktok", [128, 512], BF16, 2); hmt = Rot(fw, "hmt", [128, 512], BF16, 3)
    CT = fw.sb("CT", [128, 4, 129]); CTb = fw.sb("CTb", [128, 4, 129], BF16); mask01 = fw.sb("mask01", [128, 128])
    fw.op("pool", lambda e: e.tensor_scalar(mask01[:, :], maskb[:, :], 1.0 / 30000.0, 1.0, ALU.mult, ALU.add), [maskb], [mask01])
    fw.op("pool", lambda e: e.memset(CT[:], 0.0), [], [CT])
    fw.op("pool", lambda e: e.memset(CTb[:], 0.0), [], [CTb])
    ej = Rot(fw, "ej", [128, 128], BF16, 2)

    def epilogue(R, nums, num_bufs, den_ap, den_buf, emt_ap, emt_buf, gs, hm, f_ap=None, f_buf=None):
        s_ = stat()
        fw.op("dve", lambda e: e.memset(s_[:R, 0:16], 0.0), [], [s_])
        for h in range(4):
            j_ = ej()
            fw.op("act", lambda e, h=h, j_=j_: e.activation(j_[:R, :], nums[h], AF.Square, accum_out=s_[:R, h:h + 1]), num_bufs, [j_, s_])
        if den_ap is not None:
            fw.op("dve", lambda e: e.tensor_scalar_mul(s_[:R, 8:12], den_ap, -1.0), [den_buf, s_], [s_])
            fw.op("dve", lambda e: e.tensor_tensor(s_[:R, 4:8], s_[:R, 8:12], den_ap, ALU.max), [den_buf, s_], [s_])
            if f_ap is not None:
                fw.op("dve", lambda e: e.tensor_tensor(s_[:R, 4:8], s_[:R, 4:8], f_ap, ALU.mult), [f_buf, s_], [s_])
            fw.op("dve", lambda e: e.tensor_tensor(s_[:R, 4:8], s_[:R, 4:8], emt_ap, ALU.max), [emt_buf, s_], [s_])
            fw.op("dve", lambda e: e.reciprocal(s_[:R, 4:8], s_[:R, 4:8]), [s_], [s_])
            if f_ap is not None:
                fw.op("dve", lambda e: e.tensor_tensor(s_[:R, 4:8], s_[:R, 4:8], f_ap, ALU.mult), [f_buf, s_], [s_])
            fw.op("dve", lambda e: e.tensor_tensor(s_[:R, 8:12], s_[:R, 4:8], s_[:R, 4:8], ALU.mult), [s_], [s_])
            fw.op("dve", lambda e: e.tensor_tensor(s_[:R, 0:4], s_[:R, 0:4], s_[:R, 8:12], ALU.mult), [s_], [s_])
        rstd_of(s_[:R, 0:4], s_[:R, 12:16], s_, R, 128)
        if den_ap is not None:
            fw.op("dve", lambda e: e.tensor_tensor(s_[:R, 12:16], s_[:R, 12:16], s_[:R, 4:8], ALU.mult), [s_], [s_])
        for h in range(4):
            fw.op("dve", lambda e, h=h: e.scalar_tensor_tensor(hm[:R, h * 128:(h + 1) * 128], nums[h], s_[:R, 12 + h:13 + h], gs[:R, h * 128:(h + 1) * 128], ALU.mult, ALU.mult),
                  num_bufs + [s_, gs], [hm])

    def hm_to_T(hm, R, tt, base):
        p_ = PB()
        for h in range(4):
            tr(p_, p_[:, h * 128:h * 128 + R], hm, hm[:R, h * 128:(h + 1) * 128], R, h == 3)
        hs_ = hstage()
        fw.op("act", lambda e: e.activation(hs_[:, :, :R], p_[:, 0:512].rearrange("p (h t) -> p h t", h=4)[:, :, :R], AF.Copy), [p_], [hs_])
        fw.dma("sp", hmrT_d, hmrT_d[:, base:base + 4, tt * 128:tt * 128 + R], hs_, hs_[:, :, :R], join=True)

    def conv_j(mt, j, qT, kT):
        msl = slice(mt * 512, (mt + 1) * 512)
        p_ = PF("front"); pc = pcj()
        for kc in range(8):
            fw.op("pe", lambda e, kc=kc: e.matmul(p_[:, :512], winA[:, kc, j * 128:(j + 1) * 128], xnT[:, kc, msl], start=(kc == 0), stop=(kc == 7)),
                  [winA] + XV(mt), [p_], inc=(kc == 7))
        fw.op("pool", lambda e: e.tensor_copy(pc[:, 0:3], halo[:, j, :]), [halo], [pc])
        fw.op("act", lambda e: e.activation(pc[:, 3:515], p_[:, :512], AF.Copy), [p_], [pc])
        fw.op("pool", lambda e: e.tensor_copy(halo[:, j, :], pc[:, 512:515]), [pc], [halo])
        a_ = acc()
        fw.op("dve", lambda e: e.tensor_scalar(a_[:, :], pc[:, 0:512], wccol[:, j, 0:1], bccol[:, j:j + 1], ALU.mult, ALU.add), [pc, wccol, bccol], [a_])
        for k in range(1, 4):
            fw.op("dve", lambda e, k=k: e.scalar_tensor_tensor(a_[:, :], pc[:, k:k + 512], wccol[:, j, k:k + 1], a_[:, :], ALU.mult, ALU.add), [pc, wccol, a_], [a_])
        g_ = sgk()
        fw.op("act", lambda e: e.activation(g_[:, :], a_[:, :], AF.Exp, scale=-1.0), [a_], [g_])
        fw.op("act", lambda e: e.activation(g_[:, :], g_[:, :], AF.Ln, bias=1.0), [g_], [g_])
        fw.op("act", lambda e: e.activation(g_[:, :], g_[:, :], AF.Exp, scale=-1.0), [g_], [g_])
        if j < 4:
            fw.op("pool", lambda e: e.tensor_tensor(qT[:, j, :], a_[:, :], g_[:, :], ALU.mult), [a_, g_], [qT])
        else:
            fw.op("dve", lambda e: e.scalar_tensor_tensor(kT[:, j - 4, :], a_[:, :], KS, g_[:, :], ALU.mult, ALU.mult), [a_, g_], [kT])
        if mt == 3:
            fw.dma("sp", conv_po, conv_po[:, j * 128:(j + 1) * 128].rearrange("t c -> c t"), halo, halo[:, j, :], join=True, allow_slow_non_contiguous=True)

    def front(c, qT, kT):
        cc = c % 4
        csl = slice(cc * 128, (cc + 1) * 128)
        tsl = slice(c * 128, (c + 1) * 128)
        xv = [xnT_v[c]]
        ve = vext(); p_ = PF("front")
        for kc in range(8):
            fw.op("pe", lambda e, p_=p_, kc=kc: e.matmul(p_[:, :512], xnT[:, kc, tsl], winA[:, kc, 1024:1536], start=(kc == 0), stop=(kc == 7)), [winA] + xv, [p_], inc=(kc == 7))
        fw.op("act", lambda e: e.activation(ve[:, :, 0:128], p_[:, :512].rearrange("p (h v) -> p h v", h=4), AF.Copy), [p_], [ve])
        p2_ = PF("front"); so = sgo(); gs = gsig()
        for kc in range(8):
            fw.op("pe", lambda e, kc=kc: e.matmul(p2_[:, :512], xnT[:, kc, tsl], winA[:, kc, 1536:2048], start=(kc == 0), stop=(kc == 7)), [winA] + xv, [p2_], inc=(kc == 7))
        sig_gate(128, p2_[:, :512], p2_, so, gs, gmh, False)
        ps_s = PF("front")
        for h in range(4):
            mm(ps_s, ps_s[:, h * 128:(h + 1) * 128], kT, kT[:, h, csl], qT, qT[:, h, csl], True, True, h == 3)
        w_ = wts()
        fw.op("dve", lambda e: e.tensor_tensor(w_[:, :].rearrange("p (h t) -> p h t", h=4), ps_s[:, :512].rearrange("p (h t) -> p h t", h=4),
                                               mask01[:, :].unsqueeze(1).to_broadcast([128, 4, 128]), ALU.mult), [ps_s, mask01], [w_])
        v_ = vw(); kt = ktok()
        fw.op("pool", lambda e: e.tensor_tensor(v_[:, :, :], ve[:, :, :], GT[:, c, 32:36].unsqueeze(2).to_broadcast([128, 4, 129]), ALU.mult), [ve, GT], [v_])
        p2 = PB()
        for h in range(4):
            tr(p2, p2[:, h * 128:(h + 1) * 128], kT, kT[:, h, csl], 128, h == 3)
        fw.op("act", lambda e: e.activation(kt[:, :], p2[:, 0:512], AF.Copy), [p2], [kt])
        return dict(gs=gs, w_=w_, v_=v_, kt=kt, qT=qT, csl=csl)

    def back_a(c, F):
        gs, w_, v_, kt, qT, csl = F["gs"], F["w_"], F["v_"], F["kt"], F["qT"], F["csl"]
        po = [PF("back"), PF("back")]
        for h in range(4):
            pb_ = po[h // 2]; o0 = (h % 2) * 129
            mm(pb_, pb_[:, o0:o0 + 129], w_, w_[:, h * 128:(h + 1) * 128], v_, v_[:, h, :], True, False, False)
            mm(pb_, pb_[:, o0:o0 + 129], qT, qT[:, h, csl], CTb, CTb[:, h, :], False, True, h % 2 == 1)
        pcs = [PF("back"), PF("back")]
        for h in range(4):
            pb_ = pcs[h // 2]; o0 = (h % 2) * 129
            mm(pb_, pb_[:, o0:o0 + 129], kt, kt[:, h * 128:(h + 1) * 128], v_, v_[:, h, :], True, True, h % 2 == 1)
        for h in range(4):
            pb_ = pcs[h // 2]; o0 = (h % 2) * 129
            fw.op("dve", lambda e, h=h, pb_=pb_, o0=o0: e.scalar_tensor_tensor(CT[:, h, :], CT[:, h, :], GT[:, c, h:h + 1], pb_[:, o0:o0 + 129], ALU.mult, ALU.add),
                  [CT, GT, pb_], [CT])
        if c + 1 < 16:
            fw.op("dve", lambda e: e.tensor_tensor(CTb[:, :, :], CT[:, :, :], GT[:, c + 1, 0:4].unsqueeze(2).to_broadcast([128, 4, 129]), ALU.mult), [CT, GT], [CTb])
        dn = stat()
        for i2 in range(2):
            fw.op("dve", lambda e, i2=i2: e.tensor_copy(dn[:, 2 * i2:2 * i2 + 2], po[i2][:, 0:258].rearrange("p (h v) -> p h v", h=2)[:, :, 128]), [po[i2]], [dn])
        nums = [po[h // 2][:, (h % 2) * 129:(h % 2) * 129 + 128] for h in range(4)]
        hm = hmt()
        epilogue(128, nums, po, dn[:, 0:4], dn, GT[:, c, 64:68], GT, gs, hm, f_ap=GT2[:, c, 0:4], f_buf=GT2)
        bg_step(1)
        return hm

    Fq = {}; Hq = {}
    qk_bufs = [(qTr(), kTr()), (qTr(), kTr())]
    prog = {"main": -1, "conv": -1}
    for j in range(8):
        conv_j(0, j, qk_bufs[0][0], qk_bufs[0][1])
    prog["conv"] = 0

    def conv_stream():
        for m in range(1, 4):
            if m >= 2:
                fw.il.helper_wait(lambda m=m: prog["main"] >= 4 * (m - 1) - 1)
            for j in range(8):
                conv_j(m, j, qk_bufs[m % 2][0], qk_bufs[m % 2][1])
            prog["conv"] = m

    pools["front"] = [0, 1]; pools["smp"] = [2]
    fw.il = Interleaver(ratio=1)
    fw.il.start(conv_stream)
    for c in range(16):
        mt = c // 4
        if c % 4 == 0:
            fw.il.main_wait(lambda mt=mt: prog["conv"] >= mt)
        cur = qk_bufs[mt % 2]
        Fq[c] = front(c, cur[0], cur[1])
        if c >= 1:
            Hq[c - 1] = back_a(c - 1, Fq.pop(c - 1))
            prog["main"] = c - 1
        if c >= 2:
            hm_to_T(Hq.pop(c - 2), 128, c - 2, 0)
    Hq[15] = back_a(15, Fq.pop(15))
    hm_to_T(Hq.pop(14), 128, 14, 0)
    hm_to_T(Hq.pop(15), 128, 15, 0)
    fw.il.drain()
    fw.il = None
    pools["front"] = [0, 1, 2]
    stop("p1c")
    ctr = fw.sb("ctr", [128, 4, 128])
    p_ = PF()
    for h in range(4):
        tr(p_, p_[:, h * 128:(h + 1) * 128], CT, CT[:, h, 0:128], 128, h == 3, f32=True)
    fw.op("act", lambda e: e.activation(ctr[:, :, :], p_[:, :512].rearrange("p (h k) -> p h k", h=4), AF.Copy), [p_], [ctr])
    fw.dma("sp", C_po, C_po[:, :, :].rearrange("h v k -> v h k"), ctr, ctr[:, :, :])
    fw.dma("sp", n_po, n_po[:, :].rearrange("h k -> k h"), CT, CT[:, :, 128], allow_slow_non_contiguous=True)

    stop("p1d")
    fw.pop_scope()
    fw.push_scope()
    sel16 = cload("sel16a", c_sel16, [16, 2048])
    hmt = Rot(fw, "hmts", [128, 512], BF16, 1); ej = Rot(fw, "ejs", [128, 128], BF16, 2)
    xs_v = [xnT_v[16]]
    ssl = slice(T, T + NS)
    us = fw.sb("us", [16, 2056])
    for blk in range(5):
        c0 = blk * 512; n = 512 if blk < 4 else 8
        wsrc = winA if blk < 4 else wgt
        p_ = PF()
        for kc in range(8):
            fw.op("pe", lambda e, p_=p_, kc=kc, c0=c0, n=n: e.matmul(p_[0:16, :n], xnT[:, kc, ssl], (winA[:, kc, c0:c0 + n] if blk < 4 else wgt[:, kc, 0:8]), start=(kc == 0), stop=(kc == 7)), [wsrc] + xs_v, [p_], inc=(kc == 7))
        fw.op("act", lambda e, p_=p_, c0=c0, n=n: e.activation(us[:, c0:c0 + n], p_[0:16, :n], AF.Copy), [p_], [us])
    ctmp = fw.sb("ctmp", [16, D]); qk_s = fw.sb("qk_s", [16, D])
    fw.push_scope()
    cs_in = fw.sb("cs_in", [16, 3, 512]); wcbc = fw.sb("wcbc", [16, 4, 512]); bcbc = fw.sb("bcbc", [16, 512])
    ca = fw.sb("ca", [16, 512])
    fw.dma("sp", conv_so, conv_so[:, 2, :], us, us[:, 0:D])
    for pc_ in range(2):
        cs_ = slice(pc_ * 512, (pc_ + 1) * 512)
        fw.dma("sp", cs_in, cs_in[:, :, :], conv_s, conv_s[:, :, cs_])
        fw.dma("sp", wcbc, wcbc[:, :, :], wcbc_d, wcbc_d[:, :, cs_])
        fw.dma("sp", bcbc, bcbc[:, :], bcbc_d, bcbc_d[:, cs_])
        fw.dma("sp", conv_so, conv_so[:, 0:2, cs_], cs_in, cs_in[:, 1:3, :], join=True)
        fw.op("dve", lambda e: e.tensor_tensor(ca[:, :], us[:, cs_], wcbc[:, 3, :], ALU.mult), [us, wcbc], [ca])
        fw.op("dve", lambda e: e.tensor_tensor(ca[:, :], ca[:, :], bcbc[:, :], ALU.add), [ca, bcbc], [ca])
        for k in range(3):
            fw.op("dve", lambda e, k=k: e.tensor_tensor(ctmp[:, 0:512], cs_in[:, k, :], wcbc[:, k, :], ALU.mult), [cs_in, wcbc], [ctmp])
            fw.op("dve", lambda e: e.tensor_tensor(ca[:, :], ca[:, :], ctmp[:, 0:512], ALU.add), [ca, ctmp], [ca])
        fw.op("act", lambda e: e.activation(ctmp[:, 0:512], ca[:, :], AF.Exp, scale=-1.0), [ca], [ctmp])
        fw.op("dve", lambda e: e.tensor_scalar_add(ctmp[:, 0:512], ctmp[:, 0:512], 1.0), [ctmp], [ctmp])
        fw.op("dve", lambda e: e.reciprocal(ctmp[:, 0:512], ctmp[:, 0:512]), [ctmp], [ctmp])
        fw.op("dve", lambda e: e.tensor_tensor(qk_s[:, cs_], ctmp[:, 0:512], ca[:, :], ALU.mult), [ctmp, ca], [qk_s])
    fw.pop_scope()
    fw.op("dve", lambda e: e.tensor_scalar_mul(qk_s[:, 512:D], qk_s[:, 512:D], KS), [qk_s], [qk_s])
    gsig_s = fw.sb("gsig_s", [16, 512])
    fw.op("act", lambda e: e.activation(gsig_s[:, :], us[:, 1536:2048], AF.Exp, scale=-1.0), [us], [gsig_s])
    fw.op("dve", lambda e: e.tensor_scalar_add(gsig_s[:, :], gsig_s[:, :], 1.0), [gsig_s], [gsig_s])
    fw.op("dve", lambda e: e.reciprocal(gsig_s[:, :], gsig_s[:, :]), [gsig_s], [gsig_s])
    fw.op("dve", lambda e: e.tensor_tensor(gsig_s[:, :], gsig_s[:, :], gmh[0:16, :], ALU.mult), [gsig_s, gmh], [gsig_s])
    sg_ = fw.sb("sg_", [16, 64]); scal = fw.sb("scal", [16, 12]); n_old = fw.sb("n_old", [16, 512]); n_new = fw.sb("n_new", [16, 512])
    fw.dma("sp", sg_, sg_[:, 8:12], m_s, m_s[:, :])
    fw.dma("sp", n_old, n_old[:, :], n_s, n_s[:, :])
    fw.op("dve", lambda e: e.tensor_tensor(sg_[:, 0:8], us[:, 2048:2056], bgbc[:, :], ALU.add), [us, bgbc], [sg_])
    fw.op("act", lambda e: e.activation(sg_[:, 4:8], sg_[:, 4:8], AF.Exp, scale=-1.0), [sg_], [sg_])
    fw.op("act", lambda e: e.activation(sg_[:, 4:8], sg_[:, 4:8], AF.Ln, bias=1.0), [sg_], [sg_])
    fw.op("dve", lambda e: e.tensor_tensor(sg_[:, 12:16], sg_[:, 8:12], sg_[:, 4:8], ALU.subtract), [sg_], [sg_])
    fw.op("dve", lambda e: e.tensor_tensor(sg_[:, 16:20], sg_[:, 12:16], sg_[:, 0:4], ALU.max), [sg_], [sg_])
    fw.dma("sp", m_so, m_so[:, :], sg_, sg_[:, 16:20])
    fw.op("dve", lambda e: e.tensor_tensor(sg_[:, 20:24], sg_[:, 12:16], sg_[:, 16:20], ALU.subtract), [sg_], [sg_])
    fw.op("act", lambda e: e.activation(scal[:, 0:4], sg_[:, 20:24], AF.Exp), [sg_], [scal])
    fw.op("dve", lambda e: e.tensor_tensor(sg_[:, 20:24], sg_[:, 0:4], sg_[:, 16:20], ALU.subtract), [sg_], [sg_])
    fw.op("act", lambda e: e.activation(scal[:, 4:8], sg_[:, 20:24], AF.Exp), [sg_], [scal])
    fw.op("act", lambda e: e.activation(sg_[:, 24:28], sg_[:, 16:20], AF.Exp, scale=-1.0), [sg_], [sg_])
    fw.op("dve", lambda e: e.tensor_tensor(ctmp[:, 0:512], qk_s[:, 0:512], qk_s[:, 512:D], ALU.mult), [qk_s], [ctmp])
    fw.op("dve", lambda e: e.tensor_reduce(sg_[:, 28:32], ctmp[:, 0:512].rearrange("p (h k) -> p h k", h=4), AX.X, ALU.add), [ctmp], [sg_])
    fw.op("dve", lambda e: e.tensor_tensor(ctmp[:, 512:D], qk_s[:, 0:512], n_old[:, :], ALU.mult), [qk_s, n_old], [ctmp])
    fw.op("dve", lambda e: e.tensor_reduce(sg_[:, 32:36], ctmp[:, 512:D].rearrange("p (h k) -> p h k", h=4), AX.X, ALU.add), [ctmp], [sg_])
    fw.op("dve", lambda e: e.tensor_tensor(scal[:, 8:12], scal[:, 4:8], sg_[:, 28:32], ALU.mult), [scal, sg_], [scal])
    fw.op("dve", lambda e: e.tensor_tensor(sg_[:, 36:40], scal[:, 0:4], sg_[:, 32:36], ALU.mult), [scal, sg_], [sg_])
    fw.op("dve", lambda e: e.tensor_tensor(sg_[:, 36:40], sg_[:, 36:40], scal[:, 8:12], ALU.add), [scal, sg_], [sg_])
    fw.op("dve", lambda e: e.tensor_tensor(n_new[:, :].rearrange("p (h k) -> p h k", h=4), n_old[:, :].rearrange("p (h k) -> p h k", h=4), scal[:, 0:4].unsqueeze(2).to_broadcast([16, 4, 128]), ALU.mult), [n_old, scal], [n_new])
    fw.op("dve", lambda e: e.tensor_tensor(ctmp[:, 0:512].rearrange("p (h k) -> p h k", h=4), qk_s[:, 512:D].rearrange("p (h k) -> p h k", h=4), scal[:, 4:8].unsqueeze(2).to_broadcast([16, 4, 128]), ALU.mult), [qk_s, scal], [ctmp])
    fw.op("dve", lambda e: e.tensor_tensor(n_new[:, :], n_new[:, :], ctmp[:, 0:512], ALU.add), [n_new, ctmp], [n_new])
    fw.dma("sp", n_so, n_so[:, :], n_new, n_new[:, :])
    vT_s = fw.sb("vT_s", [128, 4, 16]); p_ = PF()
    for h in range(4):
        tr(p_, p_[:, h * 16:(h + 1) * 16], us, us[0:16, 1024 + h * 128:1024 + (h + 1) * 128], 16, h == 3, f32=True)
    fw.op("act", lambda e: e.activation(vT_s[:, :, :], p_[:, 0:64].rearrange("p (h b) -> p h b", h=4), AF.Copy), [p_], [vT_s])
    sc_all = fw.sb("sc_all", [128, 16, 12]); Cq = fw.sb("Cq", [128, 16, 4])
    Cb = Rot(fw, "Cb", [128, 4, 128], F32, 2); Cn = Rot(fw, "Cn", [128, 4, 128], F32, 2); ct1 = Rot(fw, "ct1", [128, 512], F32, 2)
    wv = Rot(fw, "wv", [128, 4], F32, 2)
    for b in range(NS):
        psq = PF(); psk = PF(); pss = PF()
        mm(psq, psq[:, :512], sel16, sel16[:, b * 128:(b + 1) * 128], qk_s, qk_s[:, 0:512], True, True, True)
        mm(psk, psk[:, :512], sel16, sel16[:, b * 128:(b + 1) * 128], qk_s, qk_s[:, 512:D], True, True, True)
        mm(pss, pss[:, :12], sel16, sel16[:, b * 128:(b + 1) * 128], scal, scal[:, :], True, True, True)
        fw.op("act", lambda e, b=b, pss=pss: e.activation(sc_all[:, b, :], pss[:, 0:12], AF.Copy), [pss], [sc_all])
        cb_ = Cb(); cn_ = Cn(); t_ = ct1(); w_ = wv()
        fw.dma("sp", cb_, cb_[:, :, :], C_s, C_s[b].rearrange("h v k -> v h k"))
        fw.op("dve", lambda e, cb_=cb_, t_=t_, psq=psq: e.tensor_tensor(t_[:, :], cb_[:, :, :].rearrange("p h k -> p (h k)"), psq[:, :512], ALU.mult), [cb_, psq], [t_])
        fw.op("dve", lambda e, t_=t_, b=b: e.tensor_reduce(Cq[:, b, :], t_[:, :].rearrange("p (h k) -> p h k", h=4), AX.X, ALU.add), [t_], [Cq])
        fw.op("dve", lambda e, w_=w_, b=b: e.tensor_tensor(w_[:, :], vT_s[:, :, b], sc_all[:, b, 4:8], ALU.mult), [vT_s, sc_all], [w_])
        fw.op("pool", lambda e, cn_=cn_, cb_=cb_, b=b: e.tensor_tensor(cn_[:, :, :], cb_[:, :, :], sc_all[:, b, 0:4].unsqueeze(2).to_broadcast([128, 4, 128]), ALU.mult), [cb_, sc_all], [cn_])
        fw.op("dve", lambda e, t_=t_, psk=psk, w_=w_: e.tensor_tensor(t_[:, :].rearrange("p (h k) -> p h k", h=4), psk[:, :512].rearrange("p (h k) -> p h k", h=4), w_[:, :].unsqueeze(2).to_broadcast([128, 4, 128]), ALU.mult), [psk, w_], [t_])
        fw.op("pool", lambda e, cn_=cn_, t_=t_: e.tensor_tensor(cn_[:, :, :], cn_[:, :, :], t_[:, :].rearrange("p (h k) -> p h k", h=4), ALU.add), [cn_, t_], [cn_])
        fw.dma("sp", C_so, C_so[b].rearrange("h v k -> v h k"), cn_, cn_[:, :, :], join=True)
    numT = fw.sb("numT", [128, 16, 4]); nt2 = fw.sb("nt2", [128, 16, 4])
    fw.op("dve", lambda e: e.tensor_tensor(numT[:, :, :], vT_s[:, :, :].rearrange("p h b -> p b h"), sc_all[:, :, 8:12], ALU.mult), [vT_s, sc_all], [numT])
    fw.op("dve", lambda e: e.tensor_tensor(nt2[:, :, :], Cq[:, :, :], sc_all[:, :, 0:4], ALU.mult), [Cq, sc_all], [nt2])
    fw.op("dve", lambda e: e.tensor_tensor(numT[:, :, :], numT[:, :, :], nt2[:, :, :], ALU.add), [numT, nt2], [numT])
    p_ = PF()
    for h in range(4):
        tr(p_, p_[0:16, h * 128:(h + 1) * 128], numT, numT[:, :, h], 128, h == 3, f32=True)
    hm = hmt()
    epilogue(16, [p_[0:16, h * 128:(h + 1) * 128] for h in range(4)], [p_], sg_[:, 36:40], sg_, sg_[:, 24:28], sg_, gsig_s, hm)
    hm_to_T(hm, 16, 16, 0)
    fw.pop_scope()
    fw.pop_scope()

    stop("p1e")
    fw.push_scope()
    winB = fw.sb("winB", [128, 8, 2048], BF16)
    winB_v = [fw.view(winB, f"winB{i}") for i in range(4)]
    for blk in range(4):
        for k0 in range(0, 8, 4):
            fw.dma("pool", winB_v[blk], winB[:, k0:k0 + 4, blk * 512:(blk + 1) * 512], w_in,
                   w_in[k0 * 128:(k0 + 4) * 128, 2056 + blk * 512:2056 + (blk + 1) * 512].rearrange("(k p) c -> p k c", p=128), join=True)
    rvt = Rot(fw, "rvt", [128, 512], BF16, 2); gsil = Rot(fw, "gsil", [128, 512], F32, 2); sgr = Rot(fw, "sgr", [128, 512], F32, 2)
    xr = Rot(fw, "xr", [128, 512], F32, 2); rt = Rot(fw, "rt", [128, 256], F32, 4)
    rqt = Rot(fw, "rqt", [128, 512], BF16, 2); rkt = Rot(fw, "rkt", [128, 512], BF16, 2)
    rqT = Rot(fw, "rqT", [128, 512], BF16, 2); rqdT = Rot(fw, "rqdT", [128, 512], BF16, 2); rkT = Rot(fw, "rkT", [128, 512], BF16, 2)
    kdt = Rot(fw, "kdt", [128, 4, 128], BF16, 2); wts = Rot(fw, "wtsr", [128, 512], BF16, 2); hmt = Rot(fw, "hmtr", [128, 512], BF16, 3)
    Sst = fw.sb("Sst", [128, 512]); Sbf = fw.sb("Sbf", [128, 512], BF16)
    ej = Rot(fw, "ejr", [128, 128], BF16, 2)
    fw.op("pool", lambda e: e.memset(Sst[:], 0.0), [], [Sst])
    fw.op("pool", lambda e: e.memset(Sbf[:], 0.0), [], [Sbf])
    sel16 = cload("sel16b", c_sel16, [16, 2048])
    rq_s = fw.sb("rq_s", [16, 512]); rk_s = fw.sb("rk_s", [16, 512]); rv_s = fw.sb("rv_s", [16, 512]); gsil_s = fw.sb("gsil_s", [16, 512])

    def rope(R, tt, src, dst, dst_dt_bf):
        sv = src[:R, :].rearrange("p (h a j) -> p h a j", h=4, a=2)
        dv = dst[:R, :].rearrange("p (h a j) -> p h a j", h=4, a=2)
        cb_ = cosT[:R, tt, :].unsqueeze(1).to_broadcast([R, 4, 64]); sb_ = sinT[:R, tt, :].unsqueeze(1).to_broadcast([R, 4, 64])
        t1 = rt(); t2 = rt(); t3 = rt(); t4 = rt()
        v4 = lambda t: t[:R, :].rearrange("p (h j) -> p h j", h=4)
        fw.op("dve", lambda e: e.tensor_tensor(v4(t1), sv[:, :, 0, :], cb_, ALU.mult), [src, cosT], [t1])
        fw.op("pool", lambda e: e.tensor_tensor(v4(t2), sv[:, :, 1, :], sb_, ALU.mult), [src, sinT], [t2])
        fw.op("dve", lambda e: e.tensor_tensor(dv[:, :, 0, :], v4(t1), v4(t2), ALU.subtract), [t1, t2], [dst])
        fw.op("pool", lambda e: e.tensor_tensor(v4(t3), sv[:, :, 1, :], cb_, ALU.mult), [src, cosT], [t3])
        fw.op("dve", lambda e: e.tensor_tensor(v4(t4), sv[:, :, 0, :], sb_, ALU.mult), [src, sinT], [t4])
        fw.op("pool", lambda e: e.tensor_tensor(dv[:, :, 1, :], v4(t3), v4(t4), ALU.add), [t3, t4], [dst])

    def rfront(tt):
        R = rows(tt); tsl = slice(tt * 128, tt * 128 + R); xv = [xnT_v[tt]]
        F = {}
        for blk in range(4):
            p_ = PF("front")
            for kc in range(8):
                fw.op("pe", lambda e, p_=p_, kc=kc, blk=blk: e.matmul(p_[:R, :512], xnT[:, kc, tsl], winB[:, kc, blk * 512:(blk + 1) * 512], start=(kc == 0), stop=(kc == 7)), [winB_v[blk]] + xv, [p_], inc=(kc == 7))
            if blk == 0 or blk == 1:
                x_ = xr()
                fw.op("act", lambda e, p_=p_, x_=x_, blk=blk: e.activation(x_[:R, :], p_[:R, :512], AF.Copy, scale=(1.0 if blk == 0 else KS)), [p_], [x_])
                if tt < 16:
                    d_ = rqt() if blk == 0 else rkt()
                else:
                    d_ = rq_s if blk == 0 else rk_s
                rope(R, tt, x_, d_, tt < 16)
                F["rq_" if blk == 0 else "rk_"] = d_
            elif blk == 2:
                if tt < 16:
                    rv_ = rvt()
                    fw.op("act", lambda e, p_=p_, rv_=rv_: e.activation(rv_[:R, :], p_[:R, :512], AF.Copy), [p_], [rv_])
                    F["rv_"] = rv_
                else:
                    fw.op("act", lambda e, p_=p_: e.activation(rv_s[:R, :], p_[:R, :512], AF.Copy), [p_], [rv_s])
            else:
                g_ = sgr(); gs = gsil() if tt < 16 else gsil_s
                sig_gate(R, p_[:R, :512], p_, g_, gs, grh, True)
                F["gs"] = gs
        if tt == 16:
            return F
        rq_, rk_ = F["rq_"], F["rk_"]
        qT_ = rqT(); qdT_ = rqdT(); kT_ = rkT(); kd_ = kdt()
        for (src_, dstT) in ((rq_, qT_), (rk_, kT_)):
            p2 = PB()
            for h in range(4):
                tr(p2, p2[:, h * 128:(h + 1) * 128], src_, src_[:, h * 128:(h + 1) * 128], 128, h == 3)
            fw.op("act", lambda e, p2=p2, dstT=dstT: e.activation(dstT[:, :], p2[:, 0:512], AF.Copy), [p2], [dstT])
        fw.op("pool", lambda e: e.tensor_tensor(qdT_[:, :], qT_[:, :], qdec[:, :], ALU.mult), [qT_, qdec], [qdT_])
        fw.op("pool", lambda e: e.tensor_tensor(kd_[:, :, :], rk_[:, :].rearrange("p (h k) -> p h k", h=4), kdec[:, :].unsqueeze(2).to_broadcast([128, 4, 128]), ALU.mult), [rk_, kdec], [kd_])
        ps_s = PF("front")
        for h in range(4):
            mm(ps_s, ps_s[:, h * 128:(h + 1) * 128], kT_, kT_[:, h * 128:(h + 1) * 128], qT_, qT_[:, h * 128:(h + 1) * 128], True, True, h == 3)
        w_ = wts()
        fw.op("dve", lambda e: e.tensor_tensor(w_[:, :], ps_s[:, :512], decayT[:, :], ALU.mult), [ps_s, decayT], [w_])
        F.update(qdT_=qdT_, kd_=kd_, w_=w_)
        return F

    def rback(tt, F):
        rv_, gs, w_, qdT_, kd_ = F["rv_"], F["gs"], F["w_"], F["qdT_"], F["kd_"]
        ps_o = PF("back")
        for h in range(4):
            hs = slice(h * 128, (h + 1) * 128)
            mm(ps_o, ps_o[:, hs], w_, w_[:, hs], rv_, rv_[:, hs], True, False, False)
            mm(ps_o, ps_o[:, hs], qdT_, qdT_[:, hs], Sbf, Sbf[:, hs], False, True, h == 3)
        ps_c = PF("back")
        for h in range(4):
            hs = slice(h * 128, (h + 1) * 128)
            mm(ps_c, ps_c[:, hs], kd_, kd_[:, h, :], rv_, rv_[:, hs], True, True, h == 3)
        fw.op("pool", lambda e: e.tensor_tensor(Sst[:, :].rearrange("p (h v) -> p h v", h=4), Sst[:, :].rearrange("p (h v) -> p h v", h=4), cdec[:, :].unsqueeze(2).to_broadcast([128, 4, 128]), ALU.mult), [Sst, cdec], [Sst])
        fw.op("dve", lambda e: e.tensor_tensor(Sst[:, :], Sst[:, :], ps_c[:, :512], ALU.add), [Sst, ps_c], [Sst])
        hm = hmt()
        epilogue(128, [ps_o[:, h * 128:(h + 1) * 128] for h in range(4)], [ps_o], None, None, None, None, gs, hm)
        bg_step(1)
        return hm

    rqT_s = fw.sb("rqT_s", [128, 4, 16]); rkT_s = fw.sb("rkT_s", [128, 4, 16])
    Sb_ = Rot(fw, "Sb_", [128, 4, 128], F32, 2); Sn_ = Rot(fw, "Sn_", [128, 4, 128], F32, 2); st1 = Rot(fw, "st1", [128, 4, 128], F32, 2)
    pso = pl; oT_s = fw.sb("oT_s", [128, 16, 4])

    def sample_ret_loop():
        for (src_, dstT) in ((rq_s, rqT_s), (rk_s, rkT_s)):
            p_ = PF()
            for h in range(4):
                tr(p_, p_[:, h * 16:(h + 1) * 16], src_, src_[0:16, h * 128:(h + 1) * 128], 16, h == 3, f32=True)
            fw.op("act", lambda e, p_=p_, dstT=dstT: e.activation(dstT[:, :, :], p_[:, 0:64].rearrange("p (h b) -> p h b", h=4), AF.Copy), [p_], [dstT])
        for b in range(NS):
            psv = PF()
            mm(psv, psv[:, :512], sel16, sel16[:, b * 128:(b + 1) * 128], rv_s, rv_s[:, :], True, True, True)
            s_ = Sb_(); sn = Sn_(); t_ = st1()
            fw.dma("sp", s_, s_[:, :, :], S_s, S_s[b].rearrange("h k v -> k h v"))
            fw.op("pool", lambda e, s_=s_, t_=t_: e.tensor_tensor(t_[:, :, :], s_[:, :, :], gam[:, :].unsqueeze(2).to_broadcast([128, 4, 128]), ALU.mult), [s_, gam], [t_])
            fw.op("dve", lambda e, sn=sn, psv=psv, b=b: e.tensor_tensor(sn[:, :, :], psv[:, :512].rearrange("p (h v) -> p h v", h=4), rkT_s[:, :, b].unsqueeze(2).to_broadcast([128, 4, 128]), ALU.mult), [psv, rkT_s], [sn])
            fw.op("pool", lambda e, sn=sn, t_=t_: e.tensor_tensor(sn[:, :, :], sn[:, :, :], t_[:, :, :], ALU.add), [sn, t_], [sn])
            fw.dma("sp", S_so, S_so[b].rearrange("h k v -> k h v"), sn, sn[:, :, :], join=True)
            for h in range(4):
                mm(pso, pso[:, b * 4 + h:b * 4 + h + 1], sn, sn[:, h, :], rqT_s, rqT_s[:, h, b:b + 1], True, True, h == 3)


    pools["back"] = [3, 4]; pools["smp"] = [5]
    rfront(16)
    fw.il = Interleaver(ratio=1)
    fw.il.start(sample_ret_loop)
    Fq = {}; Hq = {}
    for tt in range(16):
        Fq[tt] = rfront(tt)
        if 1 <= tt:
            Hq[tt - 1] = rback(tt - 1, Fq.pop(tt - 1))
            fw.op("act", lambda e: e.activation(Sbf[:, :], Sst[:, :], AF.Copy), [Sst], [Sbf])
        if 2 <= tt:
            hm_to_T(Hq.pop(tt - 2), 128, tt - 2, 4)
    Hq[15] = rback(15, Fq.pop(15))
    hm_to_T(Hq.pop(14), 128, 14, 4)
    hm_to_T(Hq.pop(15), 128, 15, 4)
    fw.il.drain()
    fw.il = None
    pools["back"] = [3, 4, 5, 6]
    fw.dma("sp", S_po, S_po[:, :, :].rearrange("h k v -> k h v"), Sst, Sst[:, :].rearrange("p (h v) -> p h v", h=4))
    stop("p2a")
    fw.op("act", lambda e: e.activation(oT_s[:, :, :], pso[:, 0:64].rearrange("p (b h) -> p b h", h=4), AF.Copy), [pso], [oT_s])
    p_ = PF()
    for h in range(4):
        tr(p_, p_[0:16, h * 128:(h + 1) * 128], oT_s, oT_s[:, :, h], 128, h == 3, f32=True)
    hm = hmt()
    epilogue(16, [p_[0:16, h * 128:(h + 1) * 128] for h in range(4)], [p_], None, None, None, None, gsil_s, hm)
    hm_to_T(hm, 16, 16, 4)
    fw.pop_scope()
    fw.pop_scope()

    stop("p2b")
    bg_step(1000)
    fw.push_scope()
    wp = Rot(fw, "wp", [128, 4096], BF16, 3)
    xres = [fw.sb(f"xres{i}", [128, D]) for i in range(5)]
    xn2T = fw.sb("xn2T", [128, 8, 528], BF16); xn3T = fw.sb("xn3T", [128, 8, 528], BF16)
    qcT = fw.sb("qcT", [128, 8, 512], BF16); oT = fw.sb("oT", [128, 8, 528], BF16); oT_sv = fw.view(oT, "oT_sv"); hT = fw.sb("hT", [128, 22, 528], BF16)
    pT = Rot(fw, "pT", [128, 2, 512], BF16, 2); rd = Rot(fw, "rd", [128, 512], F32, 2); sgf = Rot(fw, "sgf", [128, 512], F32, 2)
    qc_s = fw.sb("qc_s", [16, D]); ysb = Rot(fw, "ysb", [128, D], F32, 2)
    sps = Rot(fw, "sps", [128, 16], F32, 2); aj = Rot(fw, "aj", [128, 256], F32, 2)
    rden_s = fw.sb("rden_s", [128, 16, 4])
    Kb = Rot(fw, "Kb", [128, 2048], F32, 1); Vb = Rot(fw, "Vb", [128, 2048], BF16, 2); spb = Rot(fw, "spb", [128, 8], BF16, 2)
    sel16 = cload("sel16c", c_sel16, [16, 2048])
    hin = Rot(fw, "hin", [128, 8, 128], BF16, 2)

    def wload(dram, ap):
        w_ = wp()
        k_, c_ = ap.shape[1], ap.shape[2]
        v_ = w_[:, 0:k_ * c_].rearrange("p (k c) -> p k c", k=k_)
        fw.dma("sp", w_, v_, dram, ap)
        return w_, v_

    pools["all"] = [0, 1, 2, 3]; pools["smp"] = [4, 5]

    def sample_attn():
        pden = pl; poT = pl
        for b in range(NS):
            kb = Kb(); vb = Vb(); s_ = sps(); sb16 = spb()
            kb3 = kb[:, :].rearrange("p (m c) -> p m c", m=2); vb3 = vb[:, :].rearrange("p (m c) -> p m c", m=2)
            fw.dma("pool", kb, kb3, ck_s, ck_s[b].rearrange("(m p) c -> p m c", p=128))
            fw.dma("pool", vb, vb3, cv_s, cv_s[b].rearrange("(m p) c -> p m c", p=128))
            pq = [PF(), PF()]
            for hf in range(2):
                mm(pq[hf], pq[hf][:, :512], sel16, sel16[:, b * 128:(b + 1) * 128], qc_s, qc_s[:, hf * 512:(hf + 1) * 512], True, True, True)
            fw.op("dve", lambda e, s_=s_: e.memset(s_[:, :], 0.0), [], [s_])
            for m2 in range(2):
                for h in range(4):
                    j_ = aj()
                    fw.op("dve", lambda e, j_=j_, kb=kb, m2=m2, h=h, s_=s_: e.scalar_tensor_tensor(j_[:, :], kb3[:, m2, h * 256:(h + 1) * 256], 1.0, pq[h // 2][:, (h % 2) * 256:(h % 2) * 256 + 256], ALU.mult, ALU.mult, accum_out=s_[:, m2 * 4 + h:m2 * 4 + h + 1]),
                          [kb, pq[h // 2]], [j_, s_])
            fw.op("act", lambda e, s_=s_, sb16=sb16: e.activation(sb16[:, 0:8], s_[:, 0:8], AF.Exp, scale=1.0 / 16.0), [s_], [sb16])
            for m2 in range(2):
                mm(pden, pden[:, b * 4:(b + 1) * 4], onesb, onesb[:, :], sb16, sb16[:, m2 * 4:4 + m2 * 4], m2 == 0, m2 == 1, m2 == 1)
            for ct in range(8):
                for m2 in range(2):
                    mm(poT, poT[:, 64 + ct * 16 + b:64 + ct * 16 + b + 1], vb, vb3[:, m2, ct * 128:(ct + 1) * 128], sb16, sb16[:, m2 * 4 + ct // 2:1 + m2 * 4 + ct // 2], m2 == 0, m2 == 1, (ct == 7 and m2 == 1))
        fw.op("dve", lambda e: e.reciprocal(rden_s[:, :, :], pden[:, 0:64].rearrange("p (b h) -> p b h", h=4)), [pden], [rden_s])
        for ct in range(8):
            fw.op("dve", lambda e, ct=ct: e.tensor_tensor(oT[:, ct, 512:528], poT[:, 64 + ct * 16:64 + (ct + 1) * 16], rden_s[:, :, ct // 2], ALU.mult), [poT, rden_s], [oT_sv])


    for ps_ in range(4):
        base_t = [4 * ps_ + i for i in range(4)]
        tiles_ab = base_t + ([16] if ps_ == 0 else [])
        tiles = base_t + ([16] if ps_ == 3 else [])
        ncol = 512 + (16 if ps_ == 3 else 0)
        off = lambda i: i * 128
        wo = [wload(wout_bf, wout_bf[:, :, hf * 512:(hf + 1) * 512]) for hf in range(2)]
        pend = None
        for i, tt in enumerate(tiles_ab):
            R = rows(tt); xt = xres[i]
            hi_ = hin()
            fw.dma("sp", hi_, hi_[:, :, :R], hmrT_d, hmrT_d[:, :, tt * 128:tt * 128 + R])
            if tt < 16:
                fw.dma("sp", xt, xt[:R, :], x_p, x_p[tt * 128:(tt + 1) * 128, :])
            else:
                fw.dma("sp", xt, xt[:R, :], x_s, x_s[:, :])
            for hf in range(2):
                p_ = PF()
                for kc in range(8):
                    mm(p_, p_[:R, :512], hi_, hi_[:, kc, :R], wo[hf][0], wo[hf][1][:, kc, :], kc == 0, kc == 7, kc == 7)
                fw.op("dve", lambda e, p_=p_, xt=xt, hf=hf, R=R: e.tensor_tensor(xt[:R, hf * 512:(hf + 1) * 512], xt[:R, hf * 512:(hf + 1) * 512], p_[:R, :512], ALU.add), [p_, xt], [xt])
            r_ = norm_T(xt, R, G_XATTN, xn2T, xn2T[:, :, off(i):off(i) + R], defer=2)
            if pend is not None:
                pend()
            r_[0]()
            pend = r_[1]
        pend()
        for hf in range(2):
            wq_b, wq = wload(wcq_bf, wcq_bf[:, :, hf * 512:(hf + 1) * 512])
            for c4 in range(4):
                ct = hf * 4 + c4; p_ = PF()
                for kc in range(8):
                    mm(p_, p_[:, :512], wq_b, wq[:, kc, c4 * 128:(c4 + 1) * 128], xn2T, xn2T[:, kc, 0:512], kc == 0, kc == 7, kc == 7)
                fw.op("act", lambda e, p_=p_, ct=ct: e.activation(qcT[:, ct, :], p_[:, :512], AF.Copy), [p_], [qcT])
            if ps_ == 0:
                p_ = PF()
                for kc in range(8):
                    mm(p_, p_[0:16, :512], xn2T, xn2T[:, kc, 512:528], wq_b, wq[:, kc, :], kc == 0, kc == 7, kc == 7)
                fw.op("act", lambda e, p_=p_, hf=hf: e.activation(qc_s[:, hf * 512:(hf + 1) * 512], p_[0:16, :512], AF.Copy), [p_], [qc_s])
        def att_scores(h):
            pt = pT()
            for m2 in range(2):
                p_ = PF()
                for dc in range(2):
                    mm(p_, p_[:, :512], kmemT, kmemT[:, 2 * h + dc, m2 * 128:(m2 + 1) * 128], qcT, qcT[:, 2 * h + dc, :], dc == 0, dc == 1, dc == 1)
                fw.op("act", lambda e, p_=p_, pt=pt, m2=m2: e.activation(pt[:, m2, :], p_[:, :512], AF.Exp, scale=1.0 / 16.0), [p_], [pt])
            return pt

        def att_rest(h, pt):
            pd = PF(); r_ = rd()
            for m2 in range(2):
                mm(pd, pd[:, :512], onesb, onesb[:, :], pt, pt[:, m2, :], m2 == 0, m2 == 1, m2 == 1)
            fw.op("act", lambda e: e.activation(r_[:, :], pd[:, :512], AF.Ln), [pd], [r_])
            fw.op("act", lambda e: e.activation(r_[:, :], r_[:, :], AF.Exp, scale=-1.0), [r_], [r_])
            for dc in range(2):
                p_ = PF()
                for m2 in range(2):
                    mm(p_, p_[:, :512], vmem, vmem[:, m2, (2 * h + dc) * 128:(2 * h + dc + 1) * 128], pt, pt[:, m2, :], m2 == 0, m2 == 1, m2 == 1)
                fw.op("dve", lambda e, p_=p_, dc=dc: e.tensor_tensor(oT[:, 2 * h + dc, 0:512], p_[:, :512], r_[:, :], ALU.mult), [p_, r_], [oT])

        pts = {0: att_scores(0)}
        for h in range(4):
            if h + 1 < 4:
                pts[h + 1] = att_scores(h + 1)
            att_rest(h, pts.pop(h))
        if ps_ == 0:
            fw.il = Interleaver(ratio=1)
            fw.il.start(sample_attn)
        if ps_ == 3:
            fw.il.drain()
            fw.il = None
        wo = [wload(wco_bf, wco_bf[:, :, hf * 512:(hf + 1) * 512]) for hf in range(2)]
        pend = None
        for i, tt in enumerate(tiles):
            R = rows(tt); xt = xres[i]
            for hf in range(2):
                p_ = PF()
                for kc in range(8):
                    mm(p_, p_[:R, :512], (oT_sv if tt == 16 else oT), oT[:, kc, off(i):off(i) + R], wo[hf][0], wo[hf][1][:, kc, :], kc == 0, kc == 7, kc == 7)
                fw.op("dve", lambda e, p_=p_, xt=xt, hf=hf, R=R: e.tensor_tensor(xt[:R, hf * 512:(hf + 1) * 512], xt[:R, hf * 512:(hf + 1) * 512], p_[:R, :512], ALU.add), [p_, xt], [xt])
            r_ = norm_T(xt, R, G_FFN, xn3T, xn3T[:, :, off(i):off(i) + R], defer=2)
            if pend is not None:
                pend()
            r_[0]()
            pend = r_[1]
        pend()
        for blk in range(6):
            c0 = blk * 512; n = min(512, DFF - c0)
            wg_b, wg = wload(wg_bf, wg_bf[:, :, c0:c0 + n]); wu_b, wu = wload(wu_bf, wu_bf[:, :, c0:c0 + n])
            for fi in range(n // 128):
                f = blk * 4 + fi
                pg = PF(); pu = PF(); g_ = sgf()
                for kc in range(8):
                    mm(pg, pg[:, :512], wg_b, wg[:, kc, fi * 128:(fi + 1) * 128], xn3T, xn3T[:, kc, 0:512], kc == 0, kc == 7, kc == 7)
                for kc in range(8):
                    mm(pu, pu[:, :512], wu_b, wu[:, kc, fi * 128:(fi + 1) * 128], xn3T, xn3T[:, kc, 0:512], kc == 0, kc == 7, kc == 7)
                fw.op("act", lambda e, pg=pg, g_=g_: e.activation(g_[:, :], pg[:, :512], AF.Silu), [pg], [g_])
                fw.op("dve", lambda e, pu=pu, g_=g_, f=f: e.tensor_tensor(hT[:, f, 0:512], g_[:, :], pu[:, :512], ALU.mult), [pu, g_], [hT])
                if ps_ == 3:
                    px = PF(); g2 = sgf()
                    for kc in range(8):
                        mm(px, px[:, 0:16], wg_b, wg[:, kc, fi * 128:(fi + 1) * 128], xn3T, xn3T[:, kc, 512:528], kc == 0, kc == 7, False)
                    for kc in range(8):
                        mm(px, px[:, 16:32], wu_b, wu[:, kc, fi * 128:(fi + 1) * 128], xn3T, xn3T[:, kc, 512:528], kc == 0, kc == 7, kc == 7)
                    fw.op("act", lambda e, px=px, g2=g2: e.activation(g2[:, 0:16], px[:, 0:16], AF.Silu), [px], [g2])
                    fw.op("dve", lambda e, px=px, g2=g2, f=f: e.tensor_tensor(hT[:, f, 512:528], g2[:, 0:16], px[:, 16:32], ALU.mult), [px, g2], [hT])
        for g4 in range(0, 22, 4):
            nf = min(4, 22 - g4)
            wd_b, wd = wload(wd_bf, wd_bf[:, g4:g4 + nf, :])
            for i, tt in enumerate(tiles):
                R = rows(tt); xt = xres[i]
                for hf in range(2):
                    p_ = PF()
                    for fi in range(nf):
                        mm(p_, p_[:R, :512], hT, hT[:, g4 + fi, off(i):off(i) + R], wd_b, wd[:, fi, hf * 512:(hf + 1) * 512], fi == 0, fi == nf - 1, fi == nf - 1)
                    fw.op("dve", lambda e, p_=p_, xt=xt, hf=hf, R=R: e.tensor_tensor(xt[:R, hf * 512:(hf + 1) * 512], xt[:R, hf * 512:(hf + 1) * 512], p_[:R, :512], ALU.add), [p_, xt], [xt])
        for i, tt in enumerate(tiles):
            R = rows(tt); xt = xres[i]; sq = junk(); s_ = stat(); y_ = ysb()
            fw.op("dve", lambda e, s_=s_, R=R: e.memset(s_[:R, 0:1], 0.0), [], [s_])
            fw.op("act", lambda e, sq=sq, xt=xt, s_=s_, R=R: e.activation(sq[:R, :], xt[:R, :], AF.Square, accum_out=s_[:R, 0:1]), [xt], [sq, s_])
            rstd_of(s_[:R, 0:1], s_[:R, 1:2], s_, R, D)
            fw.op("dve", lambda e, y_=y_, xt=xt, s_=s_, R=R: e.scalar_tensor_tensor(y_[:R, :], xt[:R, :], s_[:R, 1:2], gfin[:R, :], ALU.mult, ALU.mult), [xt, s_, gfin], [y_])
            if tt < 16:
                fw.dma("pool", y_p, y_p[tt * 128:(tt + 1) * 128, :], y_, y_[:R, :], join=True)
            else:
                fw.dma("pool", y_s, y_s[:, :], y_, y_[:R, :])
    fw.pop_scope()
    for o_ in outs:
        fw.wait_all("sp", [o_])
    fw.barrier()


def _consts():
    f32 = np.float32
    c = {}
    c["c_ident"] = np.eye(128, dtype=f32)
    s_ = np.arange(128)[:, None]; t_ = np.arange(128)[None, :]
    c["c_maskb"] = np.where(s_ <= t_, 0.0, -30000.0).astype(f32)
    lg = np.log1p(-np.exp2(-5.0 - np.arange(4, dtype=np.float64)))
    dec = np.where(t_ >= s_, np.exp(lg[:, None, None] * np.maximum(t_ - s_, 0)), 0.0)
    c["c_decayT"] = np.ascontiguousarray(dec.transpose(1, 0, 2).reshape(128, 512)).astype(f32)
    qd = np.exp(lg[:, None] * (np.arange(128) + 1.0))
    c["c_qdec"] = np.ascontiguousarray(np.broadcast_to(qd.reshape(1, 512), (128, 512))).astype(f32)
    c["c_kdec"] = np.ascontiguousarray(np.exp(lg[None, :] * (127.0 - np.arange(128))[:, None])).astype(f32)
    c["c_cdec"] = np.ascontiguousarray(np.broadcast_to(np.exp(lg * 128.0)[None, :], (128, 4))).astype(f32)
    c["c_gam"] = np.ascontiguousarray(np.broadcast_to(np.exp(lg)[None, :], (128, 4))).astype(f32)
    sel4 = np.zeros((4, 4, 128), f32)
    for h in range(4):
        sel4[h, h, :] = 1.0
    c["c_sel4"] = sel4.reshape(4, 512)
    sel16 = np.zeros((16, 16, 128), f32)
    for b in range(16):
        sel16[b, b, :] = 1.0
    c["c_sel16"] = sel16.reshape(16, 2048)
    inv = (np.float32(10000.0) ** (-np.arange(64, dtype=f32) / np.float32(64))).astype(f32)
    pos = np.concatenate([np.arange(T, dtype=f32), np.full(128, 16384.0, f32)])
    ang = (pos[:, None] * inv[None, :]).astype(f32)
    c["c_cos"] = np.ascontiguousarray(np.cos(ang).astype(f32).reshape(17, 128, 64).transpose(1, 0, 2))
    c["c_sin"] = np.ascontiguousarray(np.sin(ang).astype(f32).reshape(17, 128, 64).transpose(1, 0, 2))
    return c


_NC_CACHE = {}


def kernel(x_prompt, x_sample, cache_mem_k, cache_mem_v, state_mlstm_conv, state_mlstm_C, state_mlstm_n, state_mlstm_m,
           state_ret_S, mem_prompt, w_in, b_gate, w_conv, b_conv, g_mix, g_mhead, g_rhead, w_out, g_xattn, g_mem,
           w_ck, w_cv, w_cq, w_co, g_ffn, w_gate, w_up, w_down, g_final):
    f32 = np.float32
    A = lambda a: np.ascontiguousarray(np.asarray(a, dtype=f32))
    if "nc" not in _NC_CACHE:
        _NC_CACHE["nc"] = build_nc()
    nc = _NC_CACHE["nc"]
    shared = _consts()
    col = lambda g: A(g).reshape(8, 128).T
    shared["gcols"] = A(np.stack([col(g_mix[0]), col(g_xattn[0]), col(g_ffn[0]), col(g_mem[0])], axis=1))
    shared["gmh_bc"] = A(np.broadcast_to(A(g_mhead[0])[None, :], (128, 512)))
    shared["grh_bc"] = A(np.broadcast_to(A(g_rhead[0])[None, :], (128, 512)))
    shared["gfin_bc"] = A(np.broadcast_to(A(g_final)[None, :], (128, D)))
    shared["bgate_col"] = A(A(b_gate[0]).reshape(2, 4).T)
    shared["bgate_bc"] = A(np.broadcast_to(A(b_gate[0])[None, :], (16, 8)))
    shared["wconv_col"] = A(A(w_conv[0]).reshape(4, 8, 128).transpose(2, 1, 0))
    shared["bconv_col"] = A(A(b_conv[0]).reshape(8, 128).T)
    shared["wconv_bc"] = A(np.broadcast_to(A(w_conv[0])[None], (16, 4, D)))
    shared["bconv_bc"] = A(np.broadcast_to(A(b_conv[0])[None], (16, D)))
    for k_, v_ in (("w_in", w_in), ("w_out", w_out), ("w_ck", w_ck), ("w_cv", w_cv), ("w_cq", w_cq), ("w_co", w_co),
                   ("w_gate", w_gate), ("w_up", w_up), ("w_down", w_down)):
        shared[k_] = A(v_[0])
    in_maps = []
    for i in range(8):
        sl = slice(16 * i, 16 * (i + 1))
        m = dict(shared)
        m["x_p"] = A(x_prompt[i]); m["x_s"] = A(x_sample[sl, 0]); m["mem_p"] = A(mem_prompt[i])
        m["ck_s"] = A(cache_mem_k[0, sl]).reshape(16, 256, D); m["cv_s"] = A(cache_mem_v[0, sl]).reshape(16, 256, D)
        m["conv_s"] = A(state_mlstm_conv[0, sl]); m["C_s"] = A(state_mlstm_C[0, sl]); m["n_s"] = A(state_mlstm_n[0, sl]).reshape(16, 512)
        m["m_s"] = A(state_mlstm_m[0, sl]); m["S_s"] = A(state_ret_S[0, sl])
        in_maps.append(m)
    res = run_bass_kernel_spmd(nc, in_maps, core_ids=list(range(8)))
    r = res.results
    g = lambda k: [np.asarray(r[i][k], dtype=f32) for i in range(8)]
    y_prompt = np.stack(g("y_p"))
    y_sample = np.concatenate(g("y_s")).reshape(128, 1, D)
    mk = np.stack(g("mk_o")).reshape(1, 8, 256, 4, 256)
    mv = np.stack(g("mv_o")).reshape(1, 8, 256, 4, 256)
    conv_p = np.stack(g("conv_po"))[None]
    C_p = np.stack(g("C_po"))[None]
    n_p = np.stack(g("n_po"))[None]
    m_p = np.stack(g("m_po")).reshape(1, 8, 4)
    S_p = np.stack(g("S_po"))[None]
    conv_s = np.concatenate(g("conv_so"))[None]
    C_s = np.concatenate(g("C_so"))[None]
    n_s = np.concatenate(g("n_so")).reshape(1, 128, 4, 128)
    m_s = np.concatenate(g("m_so"))[None]
    S_s = np.concatenate(g("S_so"))[None]
    return (y_prompt, y_sample, mk, mv, conv_p, C_p, n_p, m_p, S_p, conv_s, C_s, n_s, m_s, S_s)
```
